# Optimizing a Trainium2 kernel written in Bass

```python
import jax, jax.numpy as jnp
from jax import lax
import numpy as np

D_MODEL = 1024
BATCH = 8
SEQ = 2048
DEPTH = 4

GRID_W = 64
N_HEADS = 8
HEAD_DIM = 64
D_ATTN = N_HEADS * HEAD_DIM
KH_MAX = 8
KW = 16
Q_BLOCK = KW
K_BLOCK = 2 * KW
POOL_WINDOWS = (2, 4, 8, 16)
POOL_GROUPS = len(POOL_WINDOWS)
D_POOL = 512
POOL_GROUP_DIM = D_POOL // POOL_GROUPS
D_FF = 2816
CONV_W = 3
PLE_DIM = 256
N_PROJ = 3 * D_ATTN + D_POOL + 2 * D_MODEL
ALPHA = (2 * DEPTH) ** 0.25
BETA = (8 * DEPTH) ** -0.25
LN_EPS = 1e-5
NEG_INF = -1e30

kernel_name = "hybrid_natten_pool_convffn_encoder"


def layer_norm(x, g, b):
    xf = x.astype(jnp.float32)
    mu = jnp.mean(xf, axis=-1, keepdims=True)
    xc = xf - mu
    var = jnp.mean(xc * xc, axis=-1, keepdims=True)
    y = xc * lax.rsqrt(var + LN_EPS)
    return (y * g.astype(jnp.float32) + b.astype(jnp.float32)).astype(x.dtype)


def neighbourhood_attention(q, k, v, rpb):
    b, s, _ = q.shape
    rows = s // GRID_W
    kh = min(KH_MAX, rows)

    def to_grid(t):
        return t.reshape(b, rows, GRID_W, N_HEADS, HEAD_DIM).transpose(0, 3, 1, 2, 4)

    q, k, v = to_grid(q), to_grid(k), to_grid(v)
    r = np.arange(rows)
    row_start = np.clip(r - kh // 2, 0, rows - kh)
    row_idx = row_start[:, None] + np.arange(kh)
    k_rows = k[:, :, row_idx]
    v_rows = v[:, :, row_idx]
    row_off = row_idx - r[:, None] + KH_MAX - 1
    scale = HEAD_DIM ** -0.5
    outs = []
    for c0q in range(0, GRID_W, Q_BLOCK):
        c0k = int(np.clip(c0q - KW // 2, 0, GRID_W - K_BLOCK))
        qc = c0q + np.arange(Q_BLOCK)
        kc = c0k + np.arange(K_BLOCK)
        col_start = np.clip(qc - KW // 2, 0, GRID_W - KW)
        valid = (kc[None, :] >= col_start[:, None]) & (kc[None, :] < col_start[:, None] + KW)
        col_off = np.clip(kc[None, :] - qc[:, None], -(KW - 1), KW - 1) + KW - 1
        bias = rpb[:, row_off[:, None, :, None], col_off[None, :, None, :]]
        qb = q[:, :, :, c0q:c0q + Q_BLOCK]
        kb = k_rows[:, :, :, :, c0k:c0k + K_BLOCK]
        vb = v_rows[:, :, :, :, c0k:c0k + K_BLOCK]
        sc = jnp.einsum('bhrqd,bhrikd->bhrqik', qb, kb).astype(jnp.float32) * scale + bias.astype(jnp.float32)
        sc = jnp.where(valid[:, None, :], sc, NEG_INF)
        pr = jax.nn.softmax(sc.reshape(b, N_HEADS, rows, Q_BLOCK, kh * K_BLOCK), axis=-1)
        pr = pr.reshape(sc.shape).astype(v.dtype)
        outs.append(jnp.einsum('bhrqik,bhrikd->bhrqd', pr, vb))
    o = jnp.concatenate(outs, axis=3)
    return o.transpose(0, 2, 3, 1, 4).reshape(b, s, D_ATTN)


def multiscale_pool(u):
    b, s, _ = u.shape
    uf = u.astype(jnp.float32)
    cs = jnp.concatenate([jnp.zeros((b, 1, D_POOL), jnp.float32), jnp.cumsum(uf, axis=1)], axis=1)
    t = np.arange(s)
    outs = []
    for g, w in enumerate(POOL_WINDOWS):
        lo = np.clip(t - w // 2, 0, s)
        hi = np.clip(t + w // 2, 0, s)
        cnt = (hi - lo).astype(np.float32)
        sl = slice(g * POOL_GROUP_DIM, (g + 1) * POOL_GROUP_DIM)
        csg = cs[:, :, sl]
        mean = (csg[:, hi] - csg[:, lo]) / cnt[None, :, None]
        outs.append(mean - uf[:, :, sl])
    return jnp.concatenate(outs, axis=-1).astype(u.dtype)


def dwconv_centred(h, w, bias):
    s = h.shape[1]
    pad = CONV_W // 2
    hp = jnp.pad(h, ((0, 0), (pad, CONV_W - 1 - pad), (0, 0)))
    y = hp[:, 0:s] * w[0]
    for j in range(1, CONV_W):
        y = y + hp[:, j:j + s] * w[j]
    return y + bias


def setup_inputs(seed: int = 0) -> dict:
    key = jax.random.key(seed)
    ks = jax.random.split(key, 24)
    f32 = jnp.float32
    nrm = lambda k, shape, sc: jax.random.normal(k, shape, f32) * sc
    L, D = DEPTH, D_MODEL
    w_in = jnp.concatenate([
        nrm(ks[0], (L, D, D_ATTN), D ** -0.5),
        nrm(ks[1], (L, D, D_ATTN), D ** -0.5),
        nrm(ks[2], (L, D, D_ATTN), D ** -0.5 * BETA),
        nrm(ks[3], (L, D, D_POOL), D ** -0.5),
        nrm(ks[4], (L, D, 2 * D), D ** -0.5),
    ], axis=-1)
    return {
        "x": jax.random.normal(ks[5], (BATCH, SEQ, D), f32),
        "p": jax.random.normal(ks[6], (DEPTH, BATCH, SEQ, PLE_DIM), f32),
        "ln_in_g": 1.0 + nrm(ks[7], (D,), 0.02),
        "ln_in_b": nrm(ks[8], (D,), 0.02),
        "w_in": w_in,
        "b_in": nrm(ks[9], (L, N_PROJ), 0.01),
        "rpb": nrm(ks[10], (L, N_HEADS, 2 * KH_MAX - 1, 2 * KW - 1), 0.02),
        "w_attn_out": nrm(ks[11], (L, D_ATTN, D), D_ATTN ** -0.5 * BETA),
        "pool_w": nrm(ks[12], (L, POOL_GROUPS, POOL_GROUP_DIM, POOL_GROUP_DIM), POOL_GROUP_DIM ** -0.5),
        "pool_scale": 1.0 + nrm(ks[13], (L, D_POOL), 0.02),
        "w_pool_out": nrm(ks[14], (L, D_POOL, D), D_POOL ** -0.5 * BETA),
        "w_mix_out": nrm(ks[15], (L, D, D), D ** -0.5 * BETA),
        "ln1_g": 1.0 + nrm(ks[16], (L, D), 0.02),
        "ln1_b": nrm(ks[17], (L, D), 0.02),
        "w_up": nrm(ks[18], (L, D, 2 * D_FF), D ** -0.5),
        "conv_w": nrm(ks[19], (L, CONV_W, D_FF), CONV_W ** -0.5),
        "conv_b": nrm(ks[20], (L, D_FF), 0.01),
        "w_down": nrm(ks[21], (L, D_FF, D), D_FF ** -0.5 * BETA),
        "w_ple_gate": nrm(ks[22], (L, D, D), D ** -0.5),
        "w_ple_proj": nrm(ks[23], (L, PLE_DIM, D), PLE_DIM ** -0.5 * BETA),
        "ln2_g": 1.0 + nrm(jax.random.fold_in(key, 100), (L, D), 0.02),
        "ln2_b": nrm(jax.random.fold_in(key, 101), (L, D), 0.02),
    }


def reference(x, p, ln_in_g, ln_in_b, w_in, b_in, rpb, w_attn_out, pool_w, pool_scale,
              w_pool_out, w_mix_out, ln1_g, ln1_b, w_up, conv_w, conv_b, w_down,
              w_ple_gate, w_ple_proj, ln2_g, ln2_b):
    b, s, _ = x.shape
    splits = [D_ATTN, 2 * D_ATTN, 3 * D_ATTN, 3 * D_ATTN + D_POOL, 3 * D_ATTN + D_POOL + D_MODEL]
    h = layer_norm(x, ln_in_g, ln_in_b)
    for i in range(DEPTH):
        proj = h @ w_in[i] + b_in[i]
        q, k, v, u_pool, g_a, g_b = jnp.split(proj, splits, axis=-1)
        y_attn = neighbourhood_attention(q, k, v, rpb[i]) @ w_attn_out[i]
        pooled = multiscale_pool(u_pool).reshape(b, s, POOL_GROUPS, POOL_GROUP_DIM)
        pooled = jnp.einsum('bsgc,gcd->bsgd', pooled, pool_w[i]).reshape(b, s, D_POOL) * pool_scale[i]
        y_pool = pooled @ w_pool_out[i]
        merged = jax.nn.sigmoid(g_a) * y_attn + jax.nn.sigmoid(g_b) * y_pool
        h = layer_norm(ALPHA * h + merged @ w_mix_out[i], ln1_g[i], ln1_b[i])
        h_val, h_gate = jnp.split(h @ w_up[i], 2, axis=-1)
        act = jax.nn.gelu(dwconv_centred(h_gate, conv_w[i], conv_b[i]), approximate=False)
        ffn = (act * h_val) @ w_down[i]
        ple = jax.nn.sigmoid(h @ w_ple_gate[i]) * (p[i] @ w_ple_proj[i])
        h = layer_norm(ALPHA * h + ffn + ple, ln2_g[i], ln2_b[i])
    return h
```

```python
import numpy as np
import concourse.bass as bass
import concourse.mybir as mybir
from concourse.bass_utils import run_bass_kernel_spmd

F32 = mybir.dt.float32
BF16 = mybir.dt.bfloat16
AF = mybir.ActivationFunctionType
ALU = mybir.AluOpType

D = 1024
T = 2048
DEPTH = 4
KC = 8
GRID_W = 64
ROWS = 32
NH = 8
HD = 64
DFF = 2816
NF = 22
PLE = 256
ALPHA = float((2 * DEPTH) ** 0.25)
LN_EPS = 1e-5
NPAR = 156
RING = 4
SLOT = 4096
MASKV = -30000.0


def _kp(w):
    K, n = w.shape
    return np.ascontiguousarray(w.reshape(K // 128, 128, n).transpose(1, 0, 2)).reshape(128, -1)


def build_weight_stream(inp, depth):
    chunks = []
    off = 0
    units = []
    for l in range(depth):
        u = {}
        w_in = np.asarray(inp["w_in"][l])

        def add(name, arr):
            nonlocal off
            arr = np.ascontiguousarray(arr, dtype=np.float32)
            assert arr.shape[0] == 128 and arr.shape[1] <= SLOT
            chunks.append(arr.reshape(-1))
            u[name] = (off, arr.shape[1])
            off += arr.size

        add("v", _kp(w_in[:, 1024:1536]))
        add("pin", _kp(w_in[:, 1536:2048]))
        pw = np.asarray(inp["pool_w"][l])
        add("pw", pw.transpose(1, 0, 2).reshape(128, 512))
        for c in range(4):
            qk = np.concatenate([w_in[:, c * 128:(c + 1) * 128],
                                 w_in[:, 512 + c * 128:512 + (c + 1) * 128]], axis=1)
            add(f"qk{c}", _kp(qk))
        wao = np.asarray(inp["w_attn_out"][l])
        wpo = np.asarray(inp["w_pool_out"][l])
        for n in range(8):
            sl = slice(n * 128, (n + 1) * 128)
            add(f"m{n}", np.concatenate([
                _kp(w_in[:, 2048 + n * 128:2048 + (n + 1) * 128]),
                _kp(w_in[:, 3072 + n * 128:3072 + (n + 1) * 128]),
                _kp(wao[:, sl]), _kp(wpo[:, sl])], axis=1))
        wmix = np.asarray(inp["w_mix_out"][l])
        for j in range(2):
            add(f"mix{j}", _kp(wmix[:, j * 512:(j + 1) * 512]))
        wup = np.asarray(inp["w_up"][l])
        for fp in range(11):
            parts = []
            for fi in range(2):
                f = 2 * fp + fi
                parts.append(_kp(wup[:, f * 128:(f + 1) * 128]))
                parts.append(_kp(wup[:, DFF + f * 128:DFF + (f + 1) * 128]))
            add(f"up{fp}", np.concatenate(parts, axis=1))
        wdn = np.asarray(inp["w_down"][l])
        wpg = np.asarray(inp["w_ple_gate"][l])
        wpp = np.asarray(inp["w_ple_proj"][l])
        for n in range(8):
            sl = slice(n * 128, (n + 1) * 128)
            add(f"dn{n}", np.concatenate([_kp(wdn[:, sl]), _kp(wpg[:, sl]), _kp(wpp[:, sl])], axis=1))
        units.append(u)
    return np.concatenate(chunks), units


def stream_schedule(depth):
    sched = []
    for l in range(depth):
        names = ["v"] + [f"qk{c}" for c in range(4)] + ["pin", "pw"]
        for hf in range(2):
            names += [f"m{n}" for n in range(8)] + ["mix0", "mix1"]
        for hf in range(2):
            names += [f"up{fp}" for fp in range(11)] + [f"dn{n}" for n in range(8)]
        sched += [(l, n) for n in names]
    return sched


def build_params(inp, depth):
    par = np.zeros((128, depth, NPAR), np.float32)
    for l in range(depth):
        par[:, l, 0:32] = np.asarray(inp["b_in"][l]).reshape(32, 128).T
        par[:, l, 32:36] = np.asarray(inp["pool_scale"][l]).reshape(4, 128).T
        par[:, l, 36:44] = np.asarray(inp["ln1_g"][l]).reshape(8, 128).T
        par[:, l, 44:52] = np.asarray(inp["ln1_b"][l]).reshape(8, 128).T
        cw = np.asarray(inp["conv_w"][l])
        for j in range(3):
            par[:, l, 52 + 22 * j:74 + 22 * j] = cw[j].reshape(22, 128).T
        par[:, l, 118:140] = np.asarray(inp["conv_b"][l]).reshape(22, 128).T
        par[:, l, 140:148] = np.asarray(inp["ln2_g"][l]).reshape(8, 128).T
        par[:, l, 148:156] = np.asarray(inp["ln2_b"][l]).reshape(8, 128).T
    par0 = np.zeros((128, 80), np.float32)
    par0[:, 0:8] = np.asarray(inp["ln_in_g"]).reshape(8, 128).T
    par0[:, 8:16] = np.asarray(inp["ln_in_b"]).reshape(8, 128).T
    for g, w in enumerate((2, 4, 8, 16)):
        tt = np.concatenate([np.arange(8), np.arange(T - 8, T)])
        lo = np.clip(tt - w // 2, 0, T)
        hi = np.clip(tt + w // 2, 0, T)
        par0[:, 16 + g * 16:16 + (g + 1) * 16] = (1.0 / (hi - lo).astype(np.float32))[None, :]
    return par.reshape(128, depth * NPAR), par0


def build_bias_tables(rpb, depth):
    rpb = np.asarray(rpb)
    krl = np.arange(2)[:, None, None, None]
    kc = np.arange(64)[None, :, None, None]
    qrl = np.arange(2)[None, None, :, None]
    qc = np.arange(64)[None, None, None, :]
    cs = np.clip(qc - 8, 0, GRID_W - 16)
    colvalid = (kc >= cs) & (kc < cs + 16)
    coloff = np.clip(kc - qc, -15, 15) + 15
    tab = np.empty((depth, 4, 128, 2, 12, 128), np.float32)
    blocks = [(d, True) for d in range(-2, 3)] + [(d, False) for d in range(-3, 4)]
    for bi, (delta, rowmask) in enumerate(blocks):
        dr = 2 * delta + krl - qrl
        valid = colvalid & np.ones_like(dr, bool)
        if rowmask:
            valid = valid & (dr >= -4) & (dr <= 3)
        ridx = np.clip(dr + 7, 0, 14)
        ridx_b = np.broadcast_to(ridx, (2, 64, 2, 64))
        cidx_b = np.broadcast_to(coloff, (2, 64, 2, 64))
        valid_b = np.broadcast_to(valid, (2, 64, 2, 64)).reshape(128, 128)
        for l in range(depth):
            for h in range(NH):
                g = rpb[l, h][ridx_b, cidx_b].reshape(128, 128)
                tab[l, h // 2, :, h % 2, bi, :] = np.where(valid_b, g, np.float32(MASKV))
    return tab.reshape(depth, 4, 128, 2 * 12 * 128)


class Sch:
    def __init__(self, nc):
        self.nc = nc
        self.eng = {"pe": nc.tensor, "act": nc.scalar, "dve": nc.vector, "pool": nc.gpsimd, "sp": nc.sync}
        self.sem = {}
        self.cnt = {}
        self.waited = {e: {} for e in self.eng}
        for e in self.eng:
            self.sem[e] = nc.alloc_semaphore("s_" + e)
            self.cnt[e] = 0
        self.lastw = {}
        self.readers = {}
        self.nwait = 0

    def dsem(self, name):
        k = "d:" + name
        if k not in self.sem:
            self.sem[k] = self.nc.alloc_semaphore("sd_" + name)
            self.cnt[k] = 0
        return k

    def _deps(self, reads, writes):
        deps = {}
        def add(tok):
            if tok is None:
                return
            s, v = tok
            if deps.get(s, 0) < v:
                deps[s] = v
        for k in reads:
            add(self.lastw.get(k))
        for k in writes:
            add(self.lastw.get(k))
            for s, v in self.readers.get(k, {}).items():
                add((s, v))
        return deps

    def _waits(self, e, deps, self_sync=False):
        w = self.waited[e]
        for s, v in deps.items():
            if s == e and not self_sync:
                continue
            if w.get(s, 0) < v:
                self.eng[e].wait_ge(self.sem[s], v)
                w[s] = v
                self.nwait += 1

    def _register(self, tok, reads, writes):
        for k in writes:
            self.lastw[k] = tok
            self.readers[k] = {}
        for k in reads:
            r = self.readers.setdefault(k, {})
            if r.get(tok[0], 0) < tok[1]:
                r[tok[0]] = tok[1]

    def op(self, e, reads, writes, fn, self_sync=False):
        self._waits(e, self._deps(reads, writes), self_sync)
        ins = fn(self.eng[e])
        self.cnt[e] += 1
        ins.then_inc(self.sem[e], 1)
        self._register((e, self.cnt[e]), reads, writes)

    def dma(self, q, dname, reads, writes, fn):
        k = self.dsem(dname)
        self._waits(q, self._deps(reads, writes))
        ins = fn(self.eng[q])
        self.cnt[k] += 16
        ins.then_inc(self.sem[k], 16)
        self._register((k, self.cnt[k]), reads, writes)

    def alias(self, old, new):
        merged = {}
        for k in old:
            t = self.lastw.get(k)
            if t is not None and merged.get(t[0], 0) < t[1]:
                merged[t[0]] = t[1]
            for s_, v in self.readers.get(k, {}).items():
                if merged.get(s_, 0) < v:
                    merged[s_] = v
        for k in new:
            self.lastw[k] = None
            self.readers[k] = dict(merged)

    def barrier(self):
        allv = {s: v for s, v in self.cnt.items() if v > 0}
        for e in ("pe", "act", "dve", "pool", "sp"):
            self._waits(e, allv)

    def final_wait(self, e, names):
        self._waits(e, {n: self.cnt[n] for n in names if self.cnt[n] > 0})


def build_nc(depth, units, ws_total, stop=None):
    nc = bass.Bass("TRN2", target_bir_lowering=False, dynamic_dma_scratch_size=4096)
    x_d = nc.dram_tensor("x", [T, D], F32, kind="ExternalInput").ap()
    p_d = nc.dram_tensor("p", [depth, T, PLE], F32, kind="ExternalInput").ap()
    ws_d = nc.dram_tensor("ws", [ws_total], F32, kind="ExternalInput").ap()
    tab_d = nc.dram_tensor("tab", [depth, 4, 128, 3072], F32, kind="ExternalInput").ap()
    par_d = nc.dram_tensor("par", [128, depth * NPAR], F32, kind="ExternalInput").ap()
    par0_d = nc.dram_tensor("par0", [128, 80], F32, kind="ExternalInput").ap()
    idn_d = nc.dram_tensor("idn", [128, 128], F32, kind="ExternalInput").ap()
    y_d = nc.dram_tensor("y", [T, D], F32, kind="ExternalOutput").ap()

    ARENA_F32 = (212800 + 12288) // 4
    arena_t = nc.alloc_sbuf_tensor("arena", [128, ARENA_F32], F32) if hasattr(nc, "alloc_sbuf_tensor") else None
    if arena_t is None:
        arena_cm = nc.sbuf_tensor("arena", [128, ARENA_F32], F32)
        arena = arena_cm.__enter__()
    else:
        arena = arena_t
    A = arena[:]

    class Alloc:
        def __init__(self):
            self.off = 0

        def take(self, nbytes):
            o = self.off
            self.off += (nbytes + 31) // 32 * 32
            assert self.off <= ARENA_F32 * 4, f"arena overflow {self.off}"
            return o

        def f32(self, n):
            o = self.take(n * 4)
            return A[:, o // 4:o // 4 + n]

        def bf16(self, n):
            o = self.take(n * 2)
            return A[:, o // 4:o // 4 + (n + 1) // 2].bitcast(BF16)

    al = Alloc()
    hT32 = al.f32(KC * T).rearrange("p (k t) -> p k t", k=KC)
    hTb = al.bf16(KC * T).rearrange("p (k t) -> p k t", k=KC)
    ring = [al.bf16(SLOT) for _ in range(RING)]
    par = al.f32(depth * NPAR).rearrange("p (l n) -> p l n", l=depth)
    par0 = al.f32(80)
    bq8 = al.f32(depth * 4).rearrange("p (l n) -> p l n", l=depth)
    identf = al.f32(128)
    identb = al.bf16(128)
    onesb = al.bf16(128)
    halo_save = al.bf16(KC * 2).rearrange("p (k t) -> p k t", k=KC)
    halo_nx = al.bf16(KC * 2).rearrange("p (k t) -> p k t", k=KC)
    halo_tmp = al.f32(KC * 2).rearrange("p (k t) -> p k t", k=KC)
    ln_tm = [al.f32(512) for _ in range(2)]
    ln_ta = [al.f32(512) for _ in range(2)]
    PHASE_BASE = al.off

    ps_cms = [nc.psum_tensor(f"ps{b}", [128, 512], F32) for b in range(8)]
    PS = [cm.__enter__()[:] for cm in ps_cms]

    S = Sch(nc)
    ops = S.op
    LNK = [("xsq", j_) for j_ in range(4)] + [("xb", j_) for j_ in range(4)]
    MIXER_KEYS = [("oT", c_, b_) for c_ in range(4) for b_ in range(4)] + [("p2", g_, b_) for g_ in range(4) for b_ in range(4)]
    ATT_KEYS = ([("v", i_) for i_ in range(16)] + [(n_, b_) for n_ in ("qA", "qB", "kT") for b_ in range(4)] + ["tab"]
                + [("PT", s_, x_) for s_ in range(2) for x_ in range(3)] + [("otok", s_) for s_ in range(2)]
                + [("rden", s_) for s_ in range(2)])
    POOL_KEYS = [("ub", b_) for b_ in range(4)] + ["tA", "tB", "pooledT", "t16"]
    M2_KEYS = [("mg", n_, b_) for n_ in range(8) for b_ in range(2)] + [("sa", r_) for r_ in range(2)] + [("sb", r_) for r_ in range(2)] + LNK
    FFN_KEYS = ([("aT", f_, b_) for f_ in range(NF) for b_ in range(2)] + [("pT", b_) for b_ in range(4)]
                + [("gbuf", r_) for r_ in range(2)] + [("cbuf", r_) for r_ in range(2)] + [("pst", j_) for j_ in range(2)] + LNK)
    ATT_PS_KEYS = [("S", s_, 2, 4) for s_ in range(2)] + [("pv", s_, 5 + s_) for s_ in range(2)] + [("pvT", s_, 5 + s_) for s_ in range(2)]

    sched = stream_schedule(depth)
    state = {"issued": 0, "next": 0, "bank": 0}

    def issue_loads(upto):
        while state["issued"] <= min(upto, len(sched) - 1):
            k = state["issued"]
            l, name = sched[k]
            off, n = units[l][name]
            slot = k % RING
            src = ws_d[off:off + 128 * n].rearrange("(p n) -> p n", p=128)
            dst = ring[slot][:, 0:n]
            S.dma("pool", f"ring{slot}", [], [("ring", slot)],
                  lambda e, dst=dst, src=src: e.dma_start(out=dst, in_=src, max_dma_last_dim=8192))
            state["issued"] += 1

    def wnext(l, name, live_prev=0):
        k = state["next"]
        assert sched[k] == (l, name), (sched[k], l, name)
        issue_loads(k + RING - 1 - live_prev)
        state["next"] += 1
        slot = k % RING
        return ring[slot], ("ring", slot)

    held = set()
    cool = {}

    def bank():
        for k_ in list(cool):
            cool[k_] -= 1
            if cool[k_] <= 0:
                del cool[k_]
        b = state["bank"]
        while b in held or b in cool:
            b = (b + 1) % 8
        state["bank"] = (b + 1) % 8
        return b

    def pk(b):
        return ("ps", b)

    def h32k(kc, blk):
        return ("h32", kc, blk)

    def hbk(kc, blk):
        return ("hb", kc, blk)

    def bs(blk):
        return slice(blk * 512, (blk + 1) * 512)

    S.dma("sp", "idn", [], ["identf"], lambda e: e.dma_start(out=identf, in_=idn_d[:]))
    S.dma("act", "par", [], ["par"], lambda e: e.dma_start(out=par.rearrange("p l n -> p (l n)"), in_=par_d[:]))
    S.dma("act", "par0", [], ["par0"], lambda e: e.dma_start(out=par0, in_=par0_d[:]))
    ops("dve", ["identf"], ["identb"], lambda e: e.tensor_copy(out=identb, in_=identf))
    ops("dve", [], ["onesb"], lambda e: e.memset(onesb, 1.0))
    for l in range(depth):
        ops("dve", ["par"], ["bq8"], lambda e, l=l: e.tensor_scalar(
            out=bq8[:, l, :], in0=par[:, l, 0:4], scalar1=0.125, scalar2=None, op0=ALU.mult))
    issue_loads(RING - 1)

    pending = []

    def drain(n=None):
        k = len(pending) if n is None else min(n, len(pending))
        for _ in range(k):
            pending.pop(0)()

    def ln_stats(blk, lnb, xb_on_act=False):
        xsq, xb, tms, tas = lnb[:4]
        kid = lnb[4][blk % len(tms)] if len(lnb) > 4 else blk % len(tms)
        t_mean = tms[blk % len(tms)]
        t_a = tas[blk % len(tms)]
        km = ("t_mean", kid)
        ka = ("t_a", kid)
        b1 = bank()
        b2 = bank()
        cols = bs(blk)
        for kc in range(KC):
            j = kc % len(xb)
            if xb_on_act:
                ops("act", [h32k(kc, blk)], [("xb", j)],
                    lambda e, kc=kc, j=j: e.activation(out=xb[j], in_=hT32[:, kc, cols], func=AF.Copy))
            else:
                ops("dve", [h32k(kc, blk)], [("xb", j)],
                    lambda e, kc=kc, j=j: e.tensor_copy(out=xb[j], in_=hT32[:, kc, cols]))
            ops("pe", [("xb", j), "onesb"], [pk(b1)],
                lambda e, kc=kc, j=j: e.matmul(PS[b1], lhsT=onesb, rhs=xb[j], start=(kc == 0), stop=(kc == KC - 1)))
            jq = kc % len(xsq)
            ops("act", [h32k(kc, blk)], [("xsq", jq)],
                lambda e, kc=kc, jq=jq: e.activation(out=xsq[jq], in_=hT32[:, kc, cols], func=AF.Square))
            ops("pe", [("xsq", jq), "onesb"], [pk(b2)],
                lambda e, kc=kc, jq=jq: e.matmul(PS[b2], lhsT=onesb, rhs=xsq[jq], start=(kc == 0), stop=(kc == KC - 1)))

        steps = [
            lambda: ops("act", [pk(b1)], [km], lambda e: e.activation(out=t_mean, in_=PS[b1], func=AF.Identity, scale=1.0 / D)),
            lambda: ops("act", [pk(b1)], [ka], lambda e: e.activation(out=t_a, in_=PS[b1], func=AF.Square, scale=1.0 / D)),
            lambda: ops("dve", [pk(b2), ka], [ka], lambda e: e.scalar_tensor_tensor(
                out=t_a, in0=PS[b2], scalar=1.0 / D, in1=t_a, op0=ALU.mult, op1=ALU.subtract)),
            lambda: ops("dve", [ka], [ka], lambda e: e.tensor_scalar(
                out=t_a, in0=t_a, scalar1=LN_EPS, scalar2=None, op0=ALU.add)),
            lambda: ops("act", [ka], [ka], lambda e: e.activation(out=t_a, in_=t_a, func=AF.Sqrt)),
            lambda: ops("dve", [ka], [ka], lambda e: e.reciprocal(out=t_a, in_=t_a)),
            lambda: ops("dve", [km, ka], [km], lambda e: e.scalar_tensor_tensor(
                out=t_mean, in0=t_mean, scalar=-1.0, in1=t_a, op0=ALU.mult, op1=ALU.mult)),
        ]
        return steps

    def run_smalls(step_lists):
        for k in range(len(step_lists[0])):
            for st in step_lists:
                st[k]()

    def ln_apply(blk, gcol, bcol, lnb, defer=False, eng="dve"):
        xsq, xb, tms, tas = lnb[:4]
        kid = lnb[4][blk % len(tms)] if len(lnb) > 4 else blk % len(tms)
        t_mean = tms[blk % len(tms)]
        t_a = tas[blk % len(tms)]
        km = ("t_mean", kid)
        ka = ("t_a", kid)
        cols = bs(blk)

        use_ps = (eng == "dve")
        st = {}

        def apply(kc):
            xk = hT32[:, kc, cols]
            if use_ps:
                if kc == 0:
                    br = bank()
                    held.add(br)
                    bn = bank()
                    held.add(bn)
                    st["br"], st["bn"] = br, bn
                    ops("act", [ka], [pk(br)], lambda e: e.activation(out=PS[br], in_=t_a, func=AF.Copy))
                    ops("act", [km], [pk(bn)], lambda e: e.activation(out=PS[bn], in_=t_mean, func=AF.Copy))
                br, bn = st["br"], st["bn"]
                ops("dve", [h32k(kc, blk), pk(br)], [h32k(kc, blk)],
                    lambda e: e.tensor_tensor(out=xk, in0=xk, in1=PS[br], op=ALU.mult))
                ops("dve", [h32k(kc, blk), pk(bn)], [h32k(kc, blk)],
                    lambda e: e.tensor_tensor(out=xk, in0=xk, in1=PS[bn], op=ALU.add))
                if kc == KC - 1:
                    held.discard(br)
                    held.discard(bn)
                    cool[br] = 10
                    cool[bn] = 10
            else:
                ops(eng, [h32k(kc, blk), ka], [h32k(kc, blk)],
                    lambda e: e.tensor_tensor(out=xk, in0=xk, in1=t_a, op=ALU.mult))
                ops(eng, [h32k(kc, blk), km], [h32k(kc, blk)],
                    lambda e: e.tensor_tensor(out=xk, in0=xk, in1=t_mean, op=ALU.add))
            ops("act", [h32k(kc, blk), "par", "par0"], [h32k(kc, blk)],
                lambda e: e.activation(out=xk, in_=xk, func=AF.Identity, scale=gcol(kc), bias=bcol(kc)))
            ops("act", [h32k(kc, blk)], [hbk(kc, blk)],
                lambda e: e.activation(out=hTb[:, kc, cols], in_=xk, func=AF.Copy))
        for kc in range(KC):
            if defer:
                pending.append(lambda kc=kc: apply(kc))
            else:
                apply(kc)

    def layer_norm(blk, gcol, bcol, lnb, defer=False):
        run_smalls([ln_stats(blk, lnb)])
        ln_apply(blk, gcol, bcol, lnb, defer)

    al.off = PHASE_BASE
    xst = [al.f32(D) for _ in range(4)]
    lnb = ([al.bf16(512) for _ in range(3)], [al.bf16(512) for _ in range(4)], ln_tm, ln_ta)
    for i in range(16):
        j = i % 4
        S.dma("sp", f"xin{j}", [], [("xst", j)],
              lambda e, i=i, j=j: e.dma_start(out=xst[j], in_=x_d[i * 128:(i + 1) * 128, :]))
        for hh in range(2):
            b = bank()

            def tr(e, j=j, hh=hh, b=b):
                ins = None
                for q in range(4):
                    kc = hh * 4 + q
                    ins = e.transpose(out=PS[b][:, q * 128:(q + 1) * 128], in_=xst[j][:, kc * 128:(kc + 1) * 128],
                                      identity=identf)
                return ins
            ops("pe", [("xst", j), "identf"], [pk(b)], tr)
            dst = hT32[:, hh * 4:hh * 4 + 4, i * 128:(i + 1) * 128]
            src = PS[b].rearrange("p (k t) -> p k t", k=4)
            eng = "act" if hh == 0 else "dve"
            if eng == "act":
                ops("act", [pk(b)], [h32k(hh * 4 + q, i // 4) for q in range(4)],
                    lambda e, dst=dst, src=src: e.activation(out=dst, in_=src, func=AF.Copy))
            else:
                ops("dve", [pk(b)], [h32k(hh * 4 + q, i // 4) for q in range(4)],
                    lambda e, dst=dst, src=src: e.tensor_copy(out=dst, in_=src))
    lnbx = (lnb[0], lnb[1], [al.f32(512) for _ in range(2)] + ln_tm, [al.f32(512) for _ in range(2)] + ln_ta,
            ["x0", "x1", 0, 1])
    run_smalls([ln_stats(blk, lnbx) for blk in range(4)])
    for blk in (0, 3, 1, 2):
        ln_apply(blk, lambda kc: par0[:, kc:kc + 1], lambda kc: par0[:, 8 + kc:9 + kc], lnbx,
                 eng=("pool" if blk == 3 else "dve"))
    prev_keys = [("xst", j_) for j_ in range(4)] + LNK + [("t_mean", "x0"), ("t_mean", "x1"), ("t_a", "x0"), ("t_a", "x1")]

    for l in range(depth if stop != 'x' else 0):
        P = par[:, l, :]

        def pc(i, P=P):
            return P[:, i:i + 1]

        al.off = PHASE_BASE
        oT = al.bf16(4 * T).rearrange("p (k t) -> p k t", k=4)
        pooled2T = al.bf16(4 * T).rearrange("p (k t) -> p k t", k=4)
        MIX_BASE = al.off

        v_aug = al.bf16(16 * 8 * 65).rearrange("p (i h e) -> p i h e", i=16, h=8)
        qA = al.bf16(T)
        qB = al.bf16(T)
        kT = al.bf16(T)
        tabt = al.bf16(2 * 12 * 128).rearrange("p (h b q) -> p h b q", h=2, b=12)
        PT = [al.bf16(2 * 5 * 128).rearrange("p (h b q) -> p h b q", h=2, b=5) for _ in range(2)]
        otok = [al.f32(128) for _ in range(2)]
        rden = [al.f32(2) for _ in range(2)]
        S.alias(prev_keys, MIXER_KEYS + ATT_KEYS)

        slot, rk = wnext(l, "v")
        ops("dve", [], [("v", i) for i in range(16)],
            lambda e: e.memset(v_aug.rearrange("p i h e -> p (i h) e")[:, :, 64:65], 1.0))
        ops("dve", [], [("qA", b_) for b_ in range(4)], lambda e: e.memset(qA[64:128, :], 0.0))
        ops("dve", [], [("qB", b_) for b_ in range(4)], lambda e: e.memset(qB[0:64, :], 0.0))
        for i in range(16):
            if i == 8:
                drain()
            elif i > 0:
                drain(2)
            b = bank()

            def mmv(e, i=i, b=b, slot=slot):
                ins = None
                for kc in range(KC):
                    ins = e.matmul(PS[b], lhsT=hTb[:, kc, i * 128:(i + 1) * 128], rhs=slot[:, kc * 512:(kc + 1) * 512],
                                   start=(kc == 0), stop=(kc == KC - 1))
                return ins
            ops("pe", [rk] + [hbk(kc, i // 4) for kc in range(KC)], [pk(b)], mmv)
            dst = v_aug[:, i, :, 0:64]
            src = PS[b].rearrange("p (h e) -> p h e", h=8)
            if i % 2 == 0:
                ops("act", [pk(b)], [("v", i)], lambda e, dst=dst, src=src: e.activation(out=dst, in_=src, func=AF.Copy))
            else:
                ops("dve", [pk(b)], [("v", i)], lambda e, dst=dst, src=src: e.tensor_copy(out=dst, in_=src))

        for c in range(4):
            slot, rk = wnext(l, f"qk{c}")
            S.dma("pool", "tab", [], ["tab"],
                  lambda e, c=c: e.dma_start(out=tabt.rearrange("p h b q -> p (h b q)"), in_=tab_d[l, c],
                                             max_dma_last_dim=8192))
            for blk in range(4):
                b = bank()

                def mmq(e, b=b, blk=blk, slot=slot, o=0):
                    ins = None
                    for kc in range(KC):
                        ins = e.matmul(PS[b], lhsT=slot[:, kc * 256 + o:kc * 256 + o + 128], rhs=hTb[:, kc, bs(blk)],
                                       start=(kc == 0), stop=(kc == KC - 1))
                    return ins
                ops("pe", [rk] + [hbk(kc, blk) for kc in range(KC)], [pk(b)], mmq)
                ops("act", [pk(b), "bq8"], [("qA", blk)], lambda e, b=b, blk=blk, c=c: e.activation(
                    out=qA[0:64, bs(blk)], in_=PS[b][0:64, :], func=AF.Identity, scale=0.125, bias=bq8[0:64, l, c:c + 1]))
                ops("act", [pk(b), "bq8"], [("qB", blk)], lambda e, b=b, blk=blk, c=c: e.activation(
                    out=qB[64:128, bs(blk)], in_=PS[b][64:128, :], func=AF.Identity, scale=0.125,
                    bias=bq8[64:128, l, c:c + 1]))
                b2 = bank()
                ops("pe", [rk] + [hbk(kc, blk) for kc in range(KC)], [pk(b2)],
                    lambda e, b2=b2, blk=blk, slot=slot: mmq(e, b2, blk, slot, 128))
                ops("dve", [pk(b2), "par"], [("kT", blk)], lambda e, b2=b2, blk=blk, c=c: e.tensor_scalar(
                    out=kT[:, bs(blk)], in0=PS[b2], scalar1=pc(4 + c), scalar2=None, op0=ALU.add))

            def tile_info(i):
                if 2 <= i <= 13:
                    return list(range(i - 2, i + 3)), 0
                if i == 0:
                    return [0, 1, 2, 3], 5 + 3
                if i == 1:
                    return [0, 1, 2, 3], 5 + 2
                if i == 14:
                    return [12, 13, 14, 15], 5 + 1
                return [12, 13, 14, 15], 5 + 0

            def emit_S(i):
                s = i % 2
                J, tb0 = tile_info(i)
                qcols = slice(i * 128, (i + 1) * 128)
                for hh, (qq, qname) in enumerate(((qA, "qA"), (qB, "qB"))):
                    bnk = 2 * s + hh

                    def f(e, hh=hh, qq=qq, bnk=bnk, J=J, tb0=tb0):
                        ins = None
                        for jj in range(4):
                            ins = e.matmul(PS[bnk][:, jj * 128:(jj + 1) * 128],
                                           lhsT=kT[:, J[jj] * 128:(J[jj] + 1) * 128], rhs=qq[:, qcols],
                                           start=True, stop=True)
                        return ins
                    ops("pe", [(qname, i // 4)] + [("kT", j // 4) for j in J[:4]], [pk(2 * s + hh)], f)
                    ops("dve", [pk(bnk), "tab"], [pk(bnk)], lambda e, hh=hh, bnk=bnk, tb0=tb0: e.tensor_tensor(
                        out=PS[bnk], in0=PS[bnk], in1=tabt[:, hh, tb0:tb0 + 4, :].rearrange("p b q -> p (b q)"), op=ALU.add))
                if len(J) == 5:
                    def f5(e, J=J, tb0=tb0, s=s):
                        reg = PS[4 + s][:, 0:256]
                        e.matmul(reg[:, 0:128], lhsT=kT[:, J[4] * 128:(J[4] + 1) * 128], rhs=qA[:, qcols],
                                 start=True, stop=True)
                        return e.matmul(reg[:, 128:256], lhsT=kT[:, J[4] * 128:(J[4] + 1) * 128], rhs=qB[:, qcols],
                                        start=True, stop=True)
                    ops("pe", [("qA", i // 4), ("qB", i // 4), ("kT", J[4] // 4)], [pk(4 + s)], f5)
                    for hh in range(2):
                        ops("dve", [pk(4 + s), "tab"], [pk(4 + s)], lambda e, s=s, tb0=tb0, hh=hh: e.tensor_tensor(
                            out=PS[4 + s][:, hh * 128:(hh + 1) * 128], in0=PS[4 + s][:, hh * 128:(hh + 1) * 128],
                            in1=tabt[:, hh, tb0 + 4, :], op=ALU.add))
                for hh in range(2):
                    bnk = 2 * s + hh
                    ops("act", [pk(2 * s + hh)], [("PT", s, hh)], lambda e, hh=hh, bnk=bnk, s=s: e.activation(
                        out=PT[s][:, hh, 0:4, :].rearrange("p b q -> p (b q)"), in_=PS[bnk], func=AF.Exp))
                if len(J) == 5:
                    ops("act", [pk(4 + s)], [("PT", s, 2)], lambda e, s=s: e.activation(
                        out=PT[s][:, :, 4, :], in_=PS[4 + s][:, 0:256].rearrange("p (h q) -> p h q", h=2),
                        func=AF.Exp))

            def emit_PV(i):
                s = i % 2
                J, _ = tile_info(i)
                pvb = PS[6 + s]
                for hh in range(2):
                    def f(e, hh=hh, J=J, s=s, pvb=pvb):
                        ins = None
                        for jj, j in enumerate(J):
                            ins = e.matmul(pvb[:, hh * 65:(hh + 1) * 65], lhsT=PT[s][:, hh, jj, :],
                                           rhs=v_aug[:, j, 2 * c + hh, :], start=(jj == 0), stop=(jj == len(J) - 1))
                        return ins
                    rd = [("PT", s, hh)] + ([("PT", s, 2)] if len(J) == 5 else []) + [("v", j) for j in J]
                    ops("pe", rd, [pk(6 + s)], f)
                ops("dve", [pk(6 + s)], [("rden", s)], lambda e, s=s, pvb=pvb: e.reciprocal(
                    out=rden[s], in_=pvb[:, 0:130].rearrange("p (h e) -> p h e", h=2)[:, :, 64]))
                for hh in range(2):
                    ops("act", [pk(6 + s), ("rden", s)], [("otok", s)], lambda e, s=s, hh=hh, pvb=pvb: e.activation(
                        out=otok[s][:, hh * 64:(hh + 1) * 64], in_=pvb[:, hh * 65:hh * 65 + 64], func=AF.Identity,
                        scale=rden[s][:, hh:hh + 1]))

            def emit_T(i):
                s = i % 2
                fb = 4 + (i + 1) % 2
                ops("pe", [("otok", s), "identf"], [pk(fb)], lambda e, s=s, fb=fb: e.transpose(
                    out=PS[fb][:, 256:384], in_=otok[s], identity=identf))
                ops("dve", [pk(fb), "par"], [("oT", c, i // 4)], lambda e, i=i, fb=fb: e.tensor_scalar(
                    out=oT[:, c, i * 128:(i + 1) * 128], in0=PS[fb][:, 256:384], scalar1=pc(8 + c), scalar2=None,
                    op0=ALU.add))

            emit_S(0)
            for i in range(16):
                if i + 1 < 16:
                    emit_S(i + 1)
                emit_PV(i)
                if i >= 1:
                    emit_T(i - 1)
            emit_T(15)

        al.off = MIX_BASE
        HL = 16
        WB = T + 2 * HL
        ubuf = al.f32(WB)
        tA = al.f32(WB)
        tB = al.f32(WB)
        pooledT = al.bf16(T)
        t16 = al.f32(16)
        S.alias(ATT_KEYS, POOL_KEYS)
        slotU, rkU = wnext(l, "pin")
        slotW, rkW = wnext(l, "pw", live_prev=1)
        ops("dve", [], [("ub", b_) for b_ in range(4)], lambda e: e.memset(ubuf, 0.0))
        def uproj_pe(g):
            bl = []
            for blk in range(4):
                b = bank()
                bl.append(b)

                def mmu(e, b=b, blk=blk, g=g):
                    ins = None
                    for kc in range(KC):
                        ins = e.matmul(PS[b], lhsT=slotU[:, kc * 512 + g * 128:kc * 512 + (g + 1) * 128],
                                       rhs=hTb[:, kc, bs(blk)], start=(kc == 0), stop=(kc == KC - 1))
                    return ins
                ops("pe", [rkU] + [hbk(kc, blk) for kc in range(KC)], [pk(b)], mmu)
            return bl

        def uproj_evac(g, bl):
            for blk in range(4):
                ops("act", [pk(bl[blk]), "par"], [("ub", blk)], lambda e, b=bl[blk], blk=blk, g=g: e.activation(
                    out=ubuf[:, HL + blk * 512:HL + (blk + 1) * 512], in_=PS[b], func=AF.Identity, bias=pc(12 + g)))

        uproj_evac(0, uproj_pe(0))
        for g in range(4):
            w = 2 << g
            nxt = uproj_pe(g + 1) if g < 3 else None
            offs = [(-1, 0), (-1, 1), (-2, 2), (-4, 4)][:g + 1]
            rng_ = [(0, T)]
            for (o0, o1) in reversed(offs[1:]):
                lo, hi = rng_[0]
                rng_.insert(0, (lo + o0, hi + o1))
            src = ubuf
            srck = [("ub", b_) for b_ in range(4)]
            bufs = [tA, tB]
            bnames = ["tA", "tB"]
            for si, ((o0, o1), (lo, hi)) in enumerate(zip(offs, rng_)):
                dstb = bufs[si % 2]
                ops("dve", srck, [bnames[si % 2]], lambda e, src=src, dstb=dstb, o0=o0, o1=o1, lo=lo, hi=hi: e.tensor_tensor(
                    out=dstb[:, HL + lo:HL + hi], in0=src[:, HL + lo + o0:HL + hi + o0],
                    in1=src[:, HL + lo + o1:HL + hi + o1], op=ALU.add))
                src = dstb
                srck = [bnames[si % 2]]
            ops("dve", srck + [("ub", b_) for b_ in range(4)], ["pooledT"], lambda e, src=src, w=w: e.scalar_tensor_tensor(
                out=pooledT, in0=src[:, HL:HL + T], scalar=1.0 / w, in1=ubuf[:, HL:HL + T],
                op0=ALU.mult, op1=ALU.subtract))
            for (c0, r0) in ((0, 0), (T - 8, 8)):
                ops("dve", srck + ["par0"], ["t16"], lambda e, src=src, c0=c0, r0=r0, g=g: e.tensor_tensor(
                    out=t16[:, 0:8], in0=src[:, HL + c0:HL + c0 + 8], in1=par0[:, 16 + g * 16 + r0:16 + g * 16 + r0 + 8],
                    op=ALU.mult))
                ops("dve", ["t16"] + [("ub", b_) for b_ in range(4)], ["pooledT"], lambda e, c0=c0: e.tensor_tensor(
                    out=pooledT[:, c0:c0 + 8], in0=t16[:, 0:8], in1=ubuf[:, HL + c0:HL + c0 + 8], op=ALU.subtract),
                    self_sync=True)
            for blk in range(4):
                b = bank()
                ops("pe", [rkW, "pooledT"], [pk(b)], lambda e, b=b, blk=blk, g=g: e.matmul(
                    PS[b], lhsT=slotW[:, g * 128:(g + 1) * 128], rhs=pooledT[:, bs(blk)], start=True, stop=True))
                ops("act", [pk(b), "par"], [("p2", g, blk)], lambda e, b=b, blk=blk, g=g: e.activation(
                    out=pooled2T[:, g, bs(blk)], in_=PS[b], func=AF.Identity, scale=pc(32 + g)))
            if nxt is not None:
                uproj_evac(g + 1, nxt)

        al.off = MIX_BASE
        merged = al.bf16(8 * 1024).rearrange("p (k t) -> p k t", k=8)
        sa = [al.f32(512) for _ in range(2)]
        sb = [al.f32(512) for _ in range(2)]
        lnb = ([al.bf16(512) for _ in range(3)], [al.bf16(512) for _ in range(4)], ln_tm, ln_ta)
        S.alias(POOL_KEYS, M2_KEYS)
        rr = 0
        for hf in range(2):
            for n in range(8):
                slot, rk = wnext(l, f"m{n}")
                for bi in range(2):
                    blk = 2 * hf + bi
                    r = rr % 2
                    rr += 1

                    def mmk(e, b, o, nk, rhs_fn, slot=slot):
                        ins = None
                        for kc in range(nk):
                            ins = e.matmul(PS[b], lhsT=slot[:, o + kc * 128:o + (kc + 1) * 128], rhs=rhs_fn(kc),
                                           start=(kc == 0), stop=(kc == nk - 1))
                        return ins
                    b0 = bank()
                    ops("pe", [rk] + [hbk(kc, blk) for kc in range(KC)], [pk(b0)],
                        lambda e, b0=b0, blk=blk: mmk(e, b0, 0, 8, lambda kc: hTb[:, kc, bs(blk)]))
                    ops("act", [pk(b0), "par"], [("sa", r)], lambda e, b0=b0, r=r, n=n: e.activation(
                        out=sa[r], in_=PS[b0], func=AF.Sigmoid, bias=pc(16 + n)))
                    b1 = bank()
                    ops("pe", [rk] + [("oT", kc, blk) for kc in range(4)], [pk(b1)],
                        lambda e, b1=b1, blk=blk: mmk(e, b1, 2048, 4, lambda kc: oT[:, kc, bs(blk)]))
                    ops("dve", [("sa", r), pk(b1)], [("sa", r)], lambda e, b1=b1, r=r: e.tensor_tensor(
                        out=sa[r], in0=sa[r], in1=PS[b1], op=ALU.mult))
                    b2 = bank()
                    ops("pe", [rk] + [hbk(kc, blk) for kc in range(KC)], [pk(b2)],
                        lambda e, b2=b2, blk=blk: mmk(e, b2, 1024, 8, lambda kc: hTb[:, kc, bs(blk)]))
                    ops("act", [pk(b2), "par"], [("sb", r)], lambda e, b2=b2, r=r, n=n: e.activation(
                        out=sb[r], in_=PS[b2], func=AF.Sigmoid, bias=pc(24 + n)))
                    b3 = bank()
                    ops("pe", [rk] + [("p2", kc, blk) for kc in range(4)], [pk(b3)],
                        lambda e, b3=b3, blk=blk: mmk(e, b3, 2560, 4, lambda kc: pooled2T[:, kc, bs(blk)]))
                    ops("dve", [("sb", r), pk(b3)], [("sb", r)], lambda e, b3=b3, r=r: e.tensor_tensor(
                        out=sb[r], in0=sb[r], in1=PS[b3], op=ALU.mult))
                    ops("dve", [("sa", r), ("sb", r)], [("mg", n, bi)], lambda e, r=r, n=n, bi=bi: e.tensor_tensor(
                        out=merged[:, n, bi * 512:(bi + 1) * 512], in0=sa[r], in1=sb[r], op=ALU.add))
                    drain(1)
            drain()
            for j in range(2):
                slot, rk = wnext(l, f"mix{j}")
                for nn in range(4):
                    n2 = 4 * j + nn
                    for bi in range(2):
                        blk = 2 * hf + bi
                        b = bank()

                        def mmx(e, b=b, nn=nn, bi=bi, slot=slot):
                            ins = None
                            for kc in range(KC):
                                ins = e.matmul(PS[b], lhsT=slot[:, kc * 512 + nn * 128:kc * 512 + (nn + 1) * 128],
                                               rhs=merged[:, kc, bi * 512:(bi + 1) * 512],
                                               start=(kc == 0), stop=(kc == KC - 1))
                            return ins
                        ops("pe", [rk] + [("mg", kc, bi) for kc in range(KC)], [pk(b)], mmx)
                        xk = hT32[:, n2, bs(blk)]
                        ops("dve", [pk(b), h32k(n2, blk)], [h32k(n2, blk)], lambda e, b=b, xk=xk: e.scalar_tensor_tensor(
                            out=xk, in0=xk, scalar=ALPHA, in1=PS[b], op0=ALU.mult, op1=ALU.add))
            run_smalls([ln_stats(2 * hf + bi, lnb) for bi in range(2)])
            if hf == 1:
                for kc in range(KC):
                    ops("dve", [h32k(kc, 2), ("t_a", 0)], [("htmp", kc)], lambda e, kc=kc: e.tensor_tensor(
                        out=halo_tmp[:, kc, :], in0=hT32[:, kc, 1024:1026], in1=ln_ta[0][:, 0:2], op=ALU.mult))
                for kc in range(KC):
                    ops("dve", [("htmp", kc), ("t_mean", 0)], [("htmp", kc)], lambda e, kc=kc: e.tensor_tensor(
                        out=halo_tmp[:, kc, :], in0=halo_tmp[:, kc, :], in1=ln_tm[0][:, 0:2], op=ALU.add),
                        self_sync=(kc == 0))
                for kc in range(KC):
                    ops("act", [("htmp", kc), "par"], ["halo_nx"], lambda e, kc=kc: e.activation(
                        out=halo_nx[:, kc, :], in_=halo_tmp[:, kc, :], func=AF.Identity, scale=pc(36 + kc), bias=pc(44 + kc)))
            for bi in range(2):
                ln_apply(2 * hf + bi, lambda kc, pc=pc: pc(36 + kc), lambda kc, pc=pc: pc(44 + kc), lnb, defer=True)
        if stop == 'm2':
            break

        al.off = PHASE_BASE
        aT = al.bf16(NF * 1024).rearrange("p (f t) -> p f t", f=NF)
        pT = al.bf16(2 * T).rearrange("p (k t) -> p k t", k=2)
        gbuf = [al.f32(1026) for _ in range(2)]
        cbuf = [al.f32(1024) for _ in range(2)]
        pst = [al.f32(2 * PLE).rearrange("p (a d) -> p a d", a=2) for _ in range(2)]
        sg = [gbuf[0][:, 0:512], gbuf[1][:, 0:512]]
        lnb = ([al.bf16(512) for _ in range(3)], [al.bf16(512) for _ in range(4)], ln_tm, ln_ta)
        S.alias(MIXER_KEYS + M2_KEYS, FFN_KEYS)
        ops("act", [hbk(kc, 1) for kc in range(KC)], ["halo_save"],
            lambda e: e.activation(out=halo_save, in_=hTb[:, :, 1022:1024], func=AF.Copy))
        def emit_p_pair(tp2):
            j = tp2 % 2
            S.dma("sp", f"pin{j}", [], [("pst", j)],
                  lambda e, tp2=tp2, j=j: e.dma_start(
                      out=pst[j], in_=p_d[l, tp2 * 256:(tp2 + 1) * 256, :].rearrange("(a p) d -> p a d", p=128)))
            b = bank()

            def trp(e, j=j, b=b):
                ins = None
                for a in range(2):
                    for k in range(2):
                        ins = e.transpose(out=PS[b][:, (2 * a + k) * 128:(2 * a + k + 1) * 128],
                                          in_=pst[j][:, a, k * 128:(k + 1) * 128], identity=identf)
                return ins
            ops("pe", [("pst", j), "identf"], [pk(b)], trp)
            for a in range(2):
                ti = 2 * tp2 + a
                ops("act", [pk(b)], [("pT", ti // 4)], lambda e, b=b, ti=ti, a=a: e.activation(
                    out=pT[:, :, ti * 128:(ti + 1) * 128],
                    in_=PS[b][:, a * 256:(a + 1) * 256].rearrange("p (k t) -> p k t", k=2), func=AF.Copy))
        rr = 0
        for hf in range(2):
            zc = 0 if hf == 0 else 1025
            for r in range(2):
                ops("dve", [], [("gbuf", r)], lambda e, r=r, zc=zc: e.memset(gbuf[r][:, zc:zc + 1], 0.0))
            for fp in range(11):
                slot, rk = wnext(l, f"up{fp}")
                for fi in range(2):
                    f_ = 2 * fp + fi
                    r = rr % 2
                    rr += 1
                    gb_ = gbuf[r]
                    cb_ = cbuf[r]
                    og = fi * 2048 + 1024
                    ov = fi * 2048
                    bG = []
                    for bi in range(2):
                        blk = 2 * hf + bi
                        b = bank()
                        bG.append(b)

                        def mmg(e, b=b, blk=blk, o=og, slot=slot):
                            ins = None
                            for kc in range(KC):
                                ins = e.matmul(PS[b], lhsT=slot[:, o + kc * 128:o + (kc + 1) * 128], rhs=hTb[:, kc, bs(blk)],
                                               start=(kc == 0), stop=(kc == KC - 1))
                            return ins
                        ops("pe", [rk] + [hbk(kc, blk) for kc in range(KC)], [pk(b)], mmg)
                    bV = []
                    for bi in range(2):
                        blk = 2 * hf + bi
                        b = bank()
                        bV.append(b)

                        def mmv2(e, b=b, blk=blk, o=ov, slot=slot):
                            ins = None
                            for kc in range(KC):
                                ins = e.matmul(PS[b], lhsT=slot[:, o + kc * 128:o + (kc + 1) * 128], rhs=hTb[:, kc, bs(blk)],
                                               start=(kc == 0), stop=(kc == KC - 1))
                            return ins
                        ops("pe", [rk] + [hbk(kc, blk) for kc in range(KC)], [pk(b)], mmv2)
                    bH = bank()

                    def mmh(e, bH=bH, o=og, slot=slot, hf=hf):
                        ins = None
                        for kc in range(KC):
                            rhs = halo_nx[:, kc, :] if hf == 0 else halo_save[:, kc, :]
                            ins = e.matmul(PS[bH][:, 0:2], lhsT=slot[:, o + kc * 128:o + (kc + 1) * 128], rhs=rhs,
                                           start=(kc == 0), stop=(kc == KC - 1))
                        return ins
                    ops("pe", [rk, "halo_save" if hf == 1 else "halo_nx"], [pk(bH)], mmh)
                    for bi in range(2):
                        ops("act", [pk(bG[bi])], [("gbuf", r)], lambda e, b=bG[bi], bi=bi, gb_=gb_: e.activation(
                            out=gb_[:, 1 + bi * 512:1 + (bi + 1) * 512], in_=PS[b], func=AF.Copy))
                    if hf == 0:
                        ops("act", [pk(bH)], [("gbuf", r)], lambda e, bH=bH, gb_=gb_: e.activation(
                            out=gb_[:, 1025:1026], in_=PS[bH][:, 0:1], func=AF.Copy))
                    else:
                        ops("act", [pk(bH)], [("gbuf", r)], lambda e, bH=bH, gb_=gb_: e.activation(
                            out=gb_[:, 0:1], in_=PS[bH][:, 1:2], func=AF.Copy))
                    for bi in range(2):
                        ops("act", [pk(bG[bi]), "par"], [pk(bG[bi])], lambda e, b=bG[bi], f_=f_: e.activation(
                            out=PS[b], in_=PS[b], func=AF.Identity, scale=pc(74 + f_), bias=pc(118 + f_)))
                    for bi in range(2):
                        ops("dve", [("gbuf", r), pk(bG[bi]), "par"], [pk(bG[bi])],
                            lambda e, b=bG[bi], bi=bi, gb_=gb_, f_=f_: e.scalar_tensor_tensor(
                                out=PS[b], in0=gb_[:, bi * 512:bi * 512 + 512], scalar=pc(52 + f_), in1=PS[b],
                                op0=ALU.mult, op1=ALU.add))
                        ops("dve", [("gbuf", r), pk(bG[bi]), "par"], [pk(bG[bi])],
                            lambda e, b=bG[bi], bi=bi, gb_=gb_, f_=f_: e.scalar_tensor_tensor(
                                out=PS[b], in0=gb_[:, 2 + bi * 512:2 + bi * 512 + 512], scalar=pc(96 + f_), in1=PS[b],
                                op0=ALU.mult, op1=ALU.add))
                        ops("act", [pk(bG[bi])], [("cbuf", r)], lambda e, b=bG[bi], bi=bi, cb_=cb_: e.activation(
                            out=cb_[:, bi * 512:(bi + 1) * 512], in_=PS[b], func=AF.Gelu))
                    for bi in range(2):
                        ops("dve", [("cbuf", r), pk(bV[bi])], [("aT", f_, bi)], lambda e, b=bV[bi], bi=bi, cb_=cb_, f_=f_: e.tensor_tensor(
                            out=aT[:, f_, bi * 512:(bi + 1) * 512], in0=cb_[:, bi * 512:(bi + 1) * 512], in1=PS[b], op=ALU.mult))
                    if hf == 0 and f_ < 16 and f_ % 2 == 1:
                        emit_p_pair(f_ // 2)
            rs = 0
            for n in range(8):
                slot, rk = wnext(l, f"dn{n}")
                for bi in range(2):
                    blk = 2 * hf + bi
                    r = rs % 2
                    rs += 1
                    bD = bank()

                    def mmd(e, bD=bD, bi=bi, slot=slot):
                        ins = None
                        for fc in range(NF):
                            ins = e.matmul(PS[bD], lhsT=slot[:, fc * 128:(fc + 1) * 128], rhs=aT[:, fc, bi * 512:(bi + 1) * 512],
                                           start=(fc == 0), stop=(fc == NF - 1))
                        return ins
                    ops("pe", [rk] + [("aT", fc, bi) for fc in range(NF)], [pk(bD)], mmd)
                    bP = bank()

                    def mmp(e, bP=bP, blk=blk, slot=slot):
                        ins = None
                        for kc in range(KC):
                            ins = e.matmul(PS[bP], lhsT=slot[:, 2816 + kc * 128:2816 + (kc + 1) * 128], rhs=hTb[:, kc, bs(blk)],
                                           start=(kc == 0), stop=(kc == KC - 1))
                        return ins
                    ops("pe", [rk] + [hbk(kc, blk) for kc in range(KC)], [pk(bP)], mmp)
                    bQ = bank()

                    def mmq2(e, bQ=bQ, blk=blk, slot=slot):
                        ins = None
                        for kc in range(2):
                            ins = e.matmul(PS[bQ], lhsT=slot[:, 3840 + kc * 128:3840 + (kc + 1) * 128],
                                           rhs=pT[:, kc, bs(blk)], start=(kc == 0), stop=(kc == 1))
                        return ins
                    ops("pe", [rk, ("pT", blk)], [pk(bQ)], mmq2)
                    ops("act", [pk(bP)], [("gbuf", r)], lambda e, bP=bP, r=r: e.activation(out=sg[r], in_=PS[bP], func=AF.Sigmoid))
                    ops("dve", [("gbuf", r), pk(bQ)], [("gbuf", r)], lambda e, bQ=bQ, r=r: e.tensor_tensor(
                        out=sg[r], in0=sg[r], in1=PS[bQ], op=ALU.mult))
                    xk = hT32[:, n, bs(blk)]
                    ops("dve", [pk(bD), h32k(n, blk)], [h32k(n, blk)], lambda e, bD=bD, xk=xk: e.scalar_tensor_tensor(
                        out=xk, in0=xk, scalar=ALPHA, in1=PS[bD], op0=ALU.mult, op1=ALU.add))
                    ops("dve", [("gbuf", r), h32k(n, blk)], [h32k(n, blk)], lambda e, r=r, xk=xk: e.tensor_tensor(
                        out=xk, in0=xk, in1=sg[r], op=ALU.add))
                    drain(1)
            drain()
            run_smalls([ln_stats(2 * hf + bi, lnb) for bi in range(2)])
            for bi in range(2):
                ln_apply(2 * hf + bi, lambda kc, pc=pc: pc(140 + kc), lambda kc, pc=pc: pc(148 + kc), lnb, defer=True)
        prev_keys = FFN_KEYS

    al.off = PHASE_BASE
    ost = [al.f32(D) for _ in range(2)]
    S.alias(prev_keys, [("ost", 0), ("ost", 1)])
    for i in range(16):
        if i == 8:
            drain()
        j = i % 2
        for hh in range(2):
            b = bank()

            def tro(e, b=b, i=i, hh=hh):
                ins = None
                for q in range(4):
                    kc = hh * 4 + q
                    ins = e.transpose(out=PS[b][:, q * 128:(q + 1) * 128], in_=hT32[:, kc, i * 128:(i + 1) * 128],
                                      identity=identf)
                return ins
            ops("pe", [h32k(hh * 4 + q, i // 4) for q in range(4)] + ["identf"], [pk(b)], tro)
            if hh == 0:
                ops("act", [pk(b)], [("ost", j)], lambda e, b=b, j=j: e.activation(out=ost[j][:, 0:512], in_=PS[b], func=AF.Copy))
            else:
                ops("dve", [pk(b)], [("ost", j)], lambda e, b=b, j=j: e.tensor_copy(out=ost[j][:, 512:1024], in_=PS[b]))
        S.dma("sp", f"out{j}", [("ost", j)], [],
              lambda e, i=i, j=j: e.dma_start(out=y_d[i * 128:(i + 1) * 128, :], in_=ost[j]))
    S.final_wait("sp", ["d:out0", "d:out1"])
    return nc


_CACHE = {}


def _prepare(inputs, depth):
    ws, units = build_weight_stream(inputs, depth)
    par, par0 = build_params(inputs, depth)
    tab = build_bias_tables(inputs["rpb"], depth)
    return ws, units, par, par0, tab


def run(inputs, depth=DEPTH, cores=8, trace=False, stop=None):
    inputs = {k: np.asarray(v) for k, v in inputs.items()}
    ws, units, par, par0, tab = _prepare(inputs, depth)
    nc = build_nc(depth, units, ws.size, stop=stop)
    idn = np.eye(128, dtype=np.float32)
    x = inputs["x"]
    p = inputs["p"]
    in_maps = []
    for c in range(cores):
        in_maps.append({
            "x": np.ascontiguousarray(x[c]), "p": np.ascontiguousarray(p[:depth, c]),
            "ws": ws, "tab": tab, "par": par, "par0": par0, "idn": idn,
        })
    res = run_bass_kernel_spmd(nc, in_maps, core_ids=list(range(cores)), trace=trace)
    out = np.stack([res.results[c]["y"] for c in range(cores)], axis=0)
    return out, res


def kernel(**inputs):
    out, _ = run(inputs, DEPTH, 8)
    return out.astype(np.float32)
```

```python
import numpy as np
import concourse.bass as bass
import concourse.mybir as mybir
from concourse.bass_utils import run_bass_kernel_spmd

F32 = mybir.dt.float32
BF16 = mybir.dt.bfloat16
AF = mybir.ActivationFunctionType
ALU = mybir.AluOpType

D = 1024
T = 2048
DEPTH = 4
KC = 8
GRID_W = 64
ROWS = 32
NH = 8
HD = 64
DFF = 2816
NF = 22
PLE = 256
ALPHA = float((2 * DEPTH) ** 0.25)
LN_EPS = 1e-5
NPAR = 156
RING = 4
SLOT = 4096
MASKV = -30000.0


def _kp(w):
    K, n = w.shape
    return np.ascontiguousarray(w.reshape(K // 128, 128, n).transpose(1, 0, 2)).reshape(128, -1)


def build_weight_stream(inp, depth):
    chunks = []
    off = 0
    units = []
    for l in range(depth):
        u = {}
        w_in = np.asarray(inp["w_in"][l])

        def add(name, arr):
            nonlocal off
            arr = np.ascontiguousarray(arr, dtype=np.float32)
            assert arr.shape[0] == 128 and arr.shape[1] <= SLOT
            chunks.append(arr.reshape(-1))
            u[name] = (off, arr.shape[1])
            off += arr.size

        add("v", _kp(w_in[:, 1024:1536]))
        add("pin", _kp(w_in[:, 1536:2048]))
        pw = np.asarray(inp["pool_w"][l])
        add("pw", pw.transpose(1, 0, 2).reshape(128, 512))
        for c in range(4):
            qk = np.concatenate([w_in[:, c * 128:(c + 1) * 128],
                                 w_in[:, 512 + c * 128:512 + (c + 1) * 128]], axis=1)
            add(f"qk{c}", _kp(qk))
        wao = np.asarray(inp["w_attn_out"][l])
        wpo = np.asarray(inp["w_pool_out"][l])
        for n in range(8):
            sl = slice(n * 128, (n + 1) * 128)
            add(f"m{n}", np.concatenate([
                _kp(w_in[:, 2048 + n * 128:2048 + (n + 1) * 128]),
                _kp(w_in[:, 3072 + n * 128:3072 + (n + 1) * 128]),
                _kp(wao[:, sl]), _kp(wpo[:, sl])], axis=1))
        wmix = np.asarray(inp["w_mix_out"][l])
        for j in range(2):
            add(f"mix{j}", _kp(wmix[:, j * 512:(j + 1) * 512]))
        wup = np.asarray(inp["w_up"][l])
        for fp in range(11):
            parts = []
            for fi in range(2):
                f = 2 * fp + fi
                parts.append(_kp(wup[:, f * 128:(f + 1) * 128]))
                parts.append(_kp(wup[:, DFF + f * 128:DFF + (f + 1) * 128]))
            add(f"up{fp}", np.concatenate(parts, axis=1))
        wdn = np.asarray(inp["w_down"][l])
        wpg = np.asarray(inp["w_ple_gate"][l])
        wpp = np.asarray(inp["w_ple_proj"][l])
        for n in range(8):
            sl = slice(n * 128, (n + 1) * 128)
            add(f"dn{n}", np.concatenate([_kp(wdn[:, sl]), _kp(wpg[:, sl]), _kp(wpp[:, sl])], axis=1))
        units.append(u)
    return np.concatenate(chunks), units


def stream_schedule(depth):
    sched = []
    for l in range(depth):
        names = ["v"] + [f"qk{c}" for c in range(4)] + ["pin", "pw"]
        for hf in range(2):
            names += [f"m{n}" for n in range(8)] + ["mix0", "mix1"]
        for hf in range(2):
            names += [f"up{fp}" for fp in range(11)] + [f"dn{n}" for n in range(8)]
        sched += [(l, n) for n in names]
    return sched


def build_params(inp, depth):
    par = np.zeros((128, depth, NPAR), np.float32)
    for l in range(depth):
        par[:, l, 0:32] = np.asarray(inp["b_in"][l]).reshape(32, 128).T
        par[:, l, 32:36] = np.asarray(inp["pool_scale"][l]).reshape(4, 128).T
        par[:, l, 36:44] = np.asarray(inp["ln1_g"][l]).reshape(8, 128).T
        par[:, l, 44:52] = np.asarray(inp["ln1_b"][l]).reshape(8, 128).T
        cw = np.asarray(inp["conv_w"][l])
        for j in range(3):
            par[:, l, 52 + 22 * j:74 + 22 * j] = cw[j].reshape(22, 128).T
        par[:, l, 118:140] = np.asarray(inp["conv_b"][l]).reshape(22, 128).T
        par[:, l, 140:148] = np.asarray(inp["ln2_g"][l]).reshape(8, 128).T
        par[:, l, 148:156] = np.asarray(inp["ln2_b"][l]).reshape(8, 128).T
    par0 = np.zeros((128, 80), np.float32)
    par0[:, 0:8] = np.asarray(inp["ln_in_g"]).reshape(8, 128).T
    par0[:, 8:16] = np.asarray(inp["ln_in_b"]).reshape(8, 128).T
    for g, w in enumerate((2, 4, 8, 16)):
        tt = np.concatenate([np.arange(8), np.arange(T - 8, T)])
        lo = np.clip(tt - w // 2, 0, T)
        hi = np.clip(tt + w // 2, 0, T)
        par0[:, 16 + g * 16:16 + (g + 1) * 16] = (1.0 / (hi - lo).astype(np.float32))[None, :]
    return par.reshape(128, depth * NPAR), par0


def build_bias_tables(rpb, depth):
    rpb = np.asarray(rpb)
    krl = np.arange(2)[:, None, None, None]
    kc = np.arange(64)[None, :, None, None]
    qrl = np.arange(2)[None, None, :, None]
    qc = np.arange(64)[None, None, None, :]
    cs = np.clip(qc - 8, 0, GRID_W - 16)
    colvalid = (kc >= cs) & (kc < cs + 16)
    coloff = np.clip(kc - qc, -15, 15) + 15
    tab = np.empty((depth, 4, 128, 2, 12, 128), np.float32)
    blocks = [(d, True) for d in range(-2, 3)] + [(d, False) for d in range(-3, 4)]
    for bi, (delta, rowmask) in enumerate(blocks):
        dr = 2 * delta + krl - qrl
        valid = colvalid & np.ones_like(dr, bool)
        if rowmask:
            valid = valid & (dr >= -4) & (dr <= 3)
        ridx = np.clip(dr + 7, 0, 14)
        ridx_b = np.broadcast_to(ridx, (2, 64, 2, 64))
        cidx_b = np.broadcast_to(coloff, (2, 64, 2, 64))
        valid_b = np.broadcast_to(valid, (2, 64, 2, 64)).reshape(128, 128)
        for l in range(depth):
            for h in range(NH):
                g = rpb[l, h][ridx_b, cidx_b].reshape(128, 128)
                tab[l, h // 2, :, h % 2, bi, :] = np.where(valid_b, g, np.float32(MASKV))
    return tab.reshape(depth, 4, 128, 2 * 12 * 128)


class Sch:
    def __init__(self, nc):
        self.nc = nc
        self.eng = {"pe": nc.tensor, "act": nc.scalar, "dve": nc.vector, "pool": nc.gpsimd, "sp": nc.sync}
        self.sem = {}
        self.cnt = {}
        self.waited = {e: {} for e in self.eng}
        for e in self.eng:
            self.sem[e] = nc.alloc_semaphore("s_" + e)
            self.cnt[e] = 0
        self.lastw = {}
        self.readers = {}
        self.nwait = 0

    def dsem(self, name):
        k = "d:" + name
        if k not in self.sem:
            self.sem[k] = self.nc.alloc_semaphore("sd_" + name)
            self.cnt[k] = 0
        return k

    def _deps(self, reads, writes):
        deps = {}
        def add(tok):
            if tok is None:
                return
            s, v = tok
            if deps.get(s, 0) < v:
                deps[s] = v
        for k in reads:
            add(self.lastw.get(k))
        for k in writes:
            add(self.lastw.get(k))
            for s, v in self.readers.get(k, {}).items():
                add((s, v))
        return deps

    def _waits(self, e, deps, self_sync=False):
        w = self.waited[e]
        for s, v in deps.items():
            if s == e and not self_sync:
                continue
            if w.get(s, 0) < v:
                self.eng[e].wait_ge(self.sem[s], v)
                w[s] = v
                self.nwait += 1

    def _register(self, tok, reads, writes):
        for k in writes:
            self.lastw[k] = tok
            self.readers[k] = {}
        for k in reads:
            r = self.readers.setdefault(k, {})
            if r.get(tok[0], 0) < tok[1]:
                r[tok[0]] = tok[1]

    def op(self, e, reads, writes, fn, self_sync=False):
        self._waits(e, self._deps(reads, writes), self_sync)
        ins = fn(self.eng[e])
        self.cnt[e] += 1
        ins.then_inc(self.sem[e], 1)
        self._register((e, self.cnt[e]), reads, writes)

    def dma(self, q, dname, reads, writes, fn):
        k = self.dsem(dname)
        self._waits(q, self._deps(reads, writes))
        ins = fn(self.eng[q])
        self.cnt[k] += 16
        ins.then_inc(self.sem[k], 16)
        self._register((k, self.cnt[k]), reads, writes)

    def alias(self, old, new):
        merged = {}
        for k in old:
            t = self.lastw.get(k)
            if t is not None and merged.get(t[0], 0) < t[1]:
                merged[t[0]] = t[1]
            for s_, v in self.readers.get(k, {}).items():
                if merged.get(s_, 0) < v:
                    merged[s_] = v
        for k in new:
            self.lastw[k] = None
            self.readers[k] = dict(merged)

    def barrier(self):
        allv = {s: v for s, v in self.cnt.items() if v > 0}
        for e in ("pe", "act", "dve", "pool", "sp"):
            self._waits(e, allv)

    def final_wait(self, e, names):
        self._waits(e, {n: self.cnt[n] for n in names if self.cnt[n] > 0})


def build_nc(depth, units, ws_total, stop=None):
    nc = bass.Bass("TRN2", target_bir_lowering=False, dynamic_dma_scratch_size=4096)
    x_d = nc.dram_tensor("x", [T, D], F32, kind="ExternalInput").ap()
    p_d = nc.dram_tensor("p", [depth, T, PLE], F32, kind="ExternalInput").ap()
    ws_d = nc.dram_tensor("ws", [ws_total], F32, kind="ExternalInput").ap()
    tab_d = nc.dram_tensor("tab", [depth, 4, 128, 3072], F32, kind="ExternalInput").ap()
    par_d = nc.dram_tensor("par", [128, depth * NPAR], F32, kind="ExternalInput").ap()
    par0_d = nc.dram_tensor("par0", [128, 80], F32, kind="ExternalInput").ap()
    idn_d = nc.dram_tensor("idn", [128, 128], F32, kind="ExternalInput").ap()
    y_d = nc.dram_tensor("y", [T, D], F32, kind="ExternalOutput").ap()

    ARENA_F32 = (212800 + 12288) // 4
    arena_t = nc.alloc_sbuf_tensor("arena", [128, ARENA_F32], F32) if hasattr(nc, "alloc_sbuf_tensor") else None
    if arena_t is None:
        arena_cm = nc.sbuf_tensor("arena", [128, ARENA_F32], F32)
        arena = arena_cm.__enter__()
    else:
        arena = arena_t
    A = arena[:]

    class Alloc:
        def __init__(self):
            self.off = 0

        def take(self, nbytes):
            o = self.off
            self.off += (nbytes + 31) // 32 * 32
            assert self.off <= ARENA_F32 * 4, f"arena overflow {self.off}"
            return o

        def f32(self, n):
            o = self.take(n * 4)
            return A[:, o // 4:o // 4 + n]

        def bf16(self, n):
            o = self.take(n * 2)
            return A[:, o // 4:o // 4 + (n + 1) // 2].bitcast(BF16)

    al = Alloc()
    hT32 = al.f32(KC * T).rearrange("p (k t) -> p k t", k=KC)
    hTb = al.bf16(KC * T).rearrange("p (k t) -> p k t", k=KC)
    ring = [al.bf16(SLOT) for _ in range(RING)]
    par = al.f32(depth * NPAR).rearrange("p (l n) -> p l n", l=depth)
    par0 = al.f32(80)
    bq8 = al.f32(depth * 4).rearrange("p (l n) -> p l n", l=depth)
    identf = al.f32(128)
    identb = al.bf16(128)
    onesb = al.bf16(128)
    halo_save = al.bf16(KC * 2).rearrange("p (k t) -> p k t", k=KC)
    halo_nx = al.bf16(KC * 2).rearrange("p (k t) -> p k t", k=KC)
    halo_tmp = al.f32(KC * 2).rearrange("p (k t) -> p k t", k=KC)
    ln_tm = [al.f32(512) for _ in range(2)]
    ln_ta = [al.f32(512) for _ in range(2)]
    PHASE_BASE = al.off

    ps_cms = [nc.psum_tensor(f"ps{b}", [128, 512], F32) for b in range(8)]
    PS = [cm.__enter__()[:] for cm in ps_cms]

    S = Sch(nc)
    ops = S.op
    LNK = [("xsq", j_) for j_ in range(4)] + [("xb", j_) for j_ in range(4)]
    MIXER_KEYS = [("oT", c_, b_) for c_ in range(4) for b_ in range(4)] + [("p2", g_, b_) for g_ in range(4) for b_ in range(4)]
    ATT_KEYS = ([("v", i_) for i_ in range(16)] + [(n_, b_) for n_ in ("qA", "qB", "kT") for b_ in range(4)] + ["tab"]
                + [("PT", s_, x_) for s_ in range(2) for x_ in range(3)] + [("otok", s_) for s_ in range(2)]
                + [("rden", s_) for s_ in range(2)])
    POOL_KEYS = [("ub", b_) for b_ in range(4)] + ["tA", "tB", "pooledT", "t16"]
    M2_KEYS = [("mg", n_, b_) for n_ in range(8) for b_ in range(2)] + [("sa", r_) for r_ in range(2)] + [("sb", r_) for r_ in range(2)] + LNK
    FFN_KEYS = ([("aT", f_, b_) for f_ in range(NF) for b_ in range(2)] + [("pT", b_) for b_ in range(4)]
                + [("gbuf", r_) for r_ in range(2)] + [("cbuf", r_) for r_ in range(2)] + [("pst", j_) for j_ in range(2)] + LNK)
    ATT_PS_KEYS = [("S", s_, 2, 4) for s_ in range(2)] + [("pv", s_, 5 + s_) for s_ in range(2)] + [("pvT", s_, 5 + s_) for s_ in range(2)]

    sched = stream_schedule(depth)
    state = {"issued": 0, "next": 0, "bank": 0}

    def issue_loads(upto):
        while state["issued"] <= min(upto, len(sched) - 1):
            k = state["issued"]
            l, name = sched[k]
            off, n = units[l][name]
            slot = k % RING
            src = ws_d[off:off + 128 * n].rearrange("(p n) -> p n", p=128)
            dst = ring[slot][:, 0:n]
            S.dma("pool", f"ring{slot}", [], [("ring", slot)],
                  lambda e, dst=dst, src=src: e.dma_start(out=dst, in_=src, max_dma_last_dim=8192))
            state["issued"] += 1

    def wnext(l, name, live_prev=0):
        k = state["next"]
        assert sched[k] == (l, name), (sched[k], l, name)
        issue_loads(k + RING - 1 - live_prev)
        state["next"] += 1
        slot = k % RING
        return ring[slot], ("ring", slot)

    held = set()
    cool = {}

    def bank():
        for k_ in list(cool):
            cool[k_] -= 1
            if cool[k_] <= 0:
                del cool[k_]
        b = state["bank"]
        while b in held or b in cool:
            b = (b + 1) % 8
        state["bank"] = (b + 1) % 8
        return b

    def pk(b):
        return ("ps", b)

    def h32k(kc, blk):
        return ("h32", kc, blk)

    def hbk(kc, blk):
        return ("hb", kc, blk)

    def bs(blk):
        return slice(blk * 512, (blk + 1) * 512)

    S.dma("sp", "idn", [], ["identf"], lambda e: e.dma_start(out=identf, in_=idn_d[:]))
    S.dma("act", "par", [], ["par"], lambda e: e.dma_start(out=par.rearrange("p l n -> p (l n)"), in_=par_d[:]))
    S.dma("act", "par0", [], ["par0"], lambda e: e.dma_start(out=par0, in_=par0_d[:]))
    ops("dve", ["identf"], ["identb"], lambda e: e.tensor_copy(out=identb, in_=identf))
    ops("dve", [], ["onesb"], lambda e: e.memset(onesb, 1.0))
    for l in range(depth):
        ops("dve", ["par"], ["bq8"], lambda e, l=l: e.tensor_scalar(
            out=bq8[:, l, :], in0=par[:, l, 0:4], scalar1=0.125, scalar2=None, op0=ALU.mult))
    issue_loads(RING - 1)

    pending = []

    def drain(n=None):
        k = len(pending) if n is None else min(n, len(pending))
        for _ in range(k):
            pending.pop(0)()

    def ln_stats(blk, lnb, xb_on_act=False):
        xsq, xb, tms, tas = lnb[:4]
        kid = lnb[4][blk % len(tms)] if len(lnb) > 4 else blk % len(tms)
        t_mean = tms[blk % len(tms)]
        t_a = tas[blk % len(tms)]
        km = ("t_mean", kid)
        ka = ("t_a", kid)
        b1 = bank()
        b2 = bank()
        cols = bs(blk)
        for kc in range(KC):
            j = kc % len(xb)
            if xb_on_act:
                ops("act", [h32k(kc, blk)], [("xb", j)],
                    lambda e, kc=kc, j=j: e.activation(out=xb[j], in_=hT32[:, kc, cols], func=AF.Copy))
            else:
                ops("dve", [h32k(kc, blk)], [("xb", j)],
                    lambda e, kc=kc, j=j: e.tensor_copy(out=xb[j], in_=hT32[:, kc, cols]))
            ops("pe", [("xb", j), "onesb"], [pk(b1)],
                lambda e, kc=kc, j=j: e.matmul(PS[b1], lhsT=onesb, rhs=xb[j], start=(kc == 0), stop=(kc == KC - 1)))
            jq = kc % len(xsq)
            ops("act", [h32k(kc, blk)], [("xsq", jq)],
                lambda e, kc=kc, jq=jq: e.activation(out=xsq[jq], in_=hT32[:, kc, cols], func=AF.Square))
            ops("pe", [("xsq", jq), "onesb"], [pk(b2)],
                lambda e, kc=kc, jq=jq: e.matmul(PS[b2], lhsT=onesb, rhs=xsq[jq], start=(kc == 0), stop=(kc == KC - 1)))

        steps = [
            lambda: ops("act", [pk(b1)], [km], lambda e: e.activation(out=t_mean, in_=PS[b1], func=AF.Identity, scale=1.0 / D)),
            lambda: ops("act", [pk(b1)], [ka], lambda e: e.activation(out=t_a, in_=PS[b1], func=AF.Square, scale=1.0 / D)),
            lambda: ops("dve", [pk(b2), ka], [ka], lambda e: e.scalar_tensor_tensor(
                out=t_a, in0=PS[b2], scalar=1.0 / D, in1=t_a, op0=ALU.mult, op1=ALU.subtract)),
            lambda: ops("dve", [ka], [ka], lambda e: e.tensor_scalar(
                out=t_a, in0=t_a, scalar1=LN_EPS, scalar2=None, op0=ALU.add)),
            lambda: ops("act", [ka], [ka], lambda e: e.activation(out=t_a, in_=t_a, func=AF.Sqrt)),
            lambda: ops("dve", [ka], [ka], lambda e: e.reciprocal(out=t_a, in_=t_a)),
            lambda: ops("dve", [km, ka], [km], lambda e: e.scalar_tensor_tensor(
                out=t_mean, in0=t_mean, scalar=-1.0, in1=t_a, op0=ALU.mult, op1=ALU.mult)),
        ]
        return steps

    def run_smalls(step_lists):
        for k in range(len(step_lists[0])):
            for st in step_lists:
                st[k]()

    def ln_apply(blk, gcol, bcol, lnb, defer=False, eng="dve"):
        xsq, xb, tms, tas = lnb[:4]
        kid = lnb[4][blk % len(tms)] if len(lnb) > 4 else blk % len(tms)
        t_mean = tms[blk % len(tms)]
        t_a = tas[blk % len(tms)]
        km = ("t_mean", kid)
        ka = ("t_a", kid)
        cols = bs(blk)

        use_ps = (eng == "dve")
        st = {}

        def apply(kc):
            xk = hT32[:, kc, cols]
            if use_ps:
                if kc == 0:
                    br = bank()
                    held.add(br)
                    bn = bank()
                    held.add(bn)
                    st["br"], st["bn"] = br, bn
                    ops("act", [ka], [pk(br)], lambda e: e.activation(out=PS[br], in_=t_a, func=AF.Copy))
                    ops("act", [km], [pk(bn)], lambda e: e.activation(out=PS[bn], in_=t_mean, func=AF.Copy))
                br, bn = st["br"], st["bn"]
                ops("dve", [h32k(kc, blk), pk(br)], [h32k(kc, blk)],
                    lambda e: e.tensor_tensor(out=xk, in0=xk, in1=PS[br], op=ALU.mult))
                ops("dve", [h32k(kc, blk), pk(bn)], [h32k(kc, blk)],
                    lambda e: e.tensor_tensor(out=xk, in0=xk, in1=PS[bn], op=ALU.add))
                if kc == KC - 1:
                    held.discard(br)
                    held.discard(bn)
                    cool[br] = 10
                    cool[bn] = 10
            else:
                ops(eng, [h32k(kc, blk), ka], [h32k(kc, blk)],
                    lambda e: e.tensor_tensor(out=xk, in0=xk, in1=t_a, op=ALU.mult))
                ops(eng, [h32k(kc, blk), km], [h32k(kc, blk)],
                    lambda e: e.tensor_tensor(out=xk, in0=xk, in1=t_mean, op=ALU.add))
            ops("act", [h32k(kc, blk), "par", "par0"], [h32k(kc, blk)],
                lambda e: e.activation(out=xk, in_=xk, func=AF.Identity, scale=gcol(kc), bias=bcol(kc)))
            ops("act", [h32k(kc, blk)], [hbk(kc, blk)],
                lambda e: e.activation(out=hTb[:, kc, cols], in_=xk, func=AF.Copy))
        for kc in range(KC):
            if defer:
                pending.append(lambda kc=kc: apply(kc))
            else:
                apply(kc)

    def layer_norm(blk, gcol, bcol, lnb, defer=False):
        run_smalls([ln_stats(blk, lnb)])
        ln_apply(blk, gcol, bcol, lnb, defer)

    al.off = PHASE_BASE
    xst = [al.f32(D) for _ in range(4)]
    lnb = ([al.bf16(512) for _ in range(3)], [al.bf16(512) for _ in range(4)], ln_tm, ln_ta)
    for i in range(16):
        j = i % 4
        S.dma("sp", f"xin{j}", [], [("xst", j)],
              lambda e, i=i, j=j: e.dma_start(out=xst[j], in_=x_d[i * 128:(i + 1) * 128, :]))
        for hh in range(2):
            b = bank()

            def tr(e, j=j, hh=hh, b=b):
                ins = None
                for q in range(4):
                    kc = hh * 4 + q
                    ins = e.transpose(out=PS[b][:, q * 128:(q + 1) * 128], in_=xst[j][:, kc * 128:(kc + 1) * 128],
                                      identity=identf)
                return ins
            ops("pe", [("xst", j), "identf"], [pk(b)], tr)
            dst = hT32[:, hh * 4:hh * 4 + 4, i * 128:(i + 1) * 128]
            src = PS[b].rearrange("p (k t) -> p k t", k=4)
            eng = "act" if hh == 0 else "dve"
            if eng == "act":
                ops("act", [pk(b)], [h32k(hh * 4 + q, i // 4) for q in range(4)],
                    lambda e, dst=dst, src=src: e.activation(out=dst, in_=src, func=AF.Copy))
            else:
                ops("dve", [pk(b)], [h32k(hh * 4 + q, i // 4) for q in range(4)],
                    lambda e, dst=dst, src=src: e.tensor_copy(out=dst, in_=src))
    lnbx = (lnb[0], lnb[1], [al.f32(512) for _ in range(2)] + ln_tm, [al.f32(512) for _ in range(2)] + ln_ta,
            ["x0", "x1", 0, 1])
    run_smalls([ln_stats(blk, lnbx) for blk in range(4)])
    for blk in (0, 3, 1, 2):
        ln_apply(blk, lambda kc: par0[:, kc:kc + 1], lambda kc: par0[:, 8 + kc:9 + kc], lnbx,
                 eng=("pool" if blk == 3 else "dve"))
    prev_keys = [("xst", j_) for j_ in range(4)] + LNK + [("t_mean", "x0"), ("t_mean", "x1"), ("t_a", "x0"), ("t_a", "x1")]

    for l in range(depth if stop != 'x' else 0):
        P = par[:, l, :]

        def pc(i, P=P):
            return P[:, i:i + 1]

        al.off = PHASE_BASE
        oT = al.bf16(4 * T).rearrange("p (k t) -> p k t", k=4)
        pooled2T = al.bf16(4 * T).rearrange("p (k t) -> p k t", k=4)
        MIX_BASE = al.off

        v_aug = al.bf16(16 * 8 * 65).rearrange("p (i h e) -> p i h e", i=16, h=8)
        qA = al.bf16(T)
        qB = al.bf16(T)
        kT = al.bf16(T)
        tabt = al.bf16(2 * 12 * 128).rearrange("p (h b q) -> p h b q", h=2, b=12)
        PT = [al.bf16(2 * 5 * 128).rearrange("p (h b q) -> p h b q", h=2, b=5) for _ in range(2)]
        otok = [al.f32(128) for _ in range(2)]
        rden = [al.f32(2) for _ in range(2)]
        S.alias(prev_keys, MIXER_KEYS + ATT_KEYS)

        slot, rk = wnext(l, "v")
        ops("dve", [], [("v", i) for i in range(16)],
            lambda e: e.memset(v_aug.rearrange("p i h e -> p (i h) e")[:, :, 64:65], 1.0))
        ops("dve", [], [("qA", b_) for b_ in range(4)], lambda e: e.memset(qA[64:128, :], 0.0))
        ops("dve", [], [("qB", b_) for b_ in range(4)], lambda e: e.memset(qB[0:64, :], 0.0))
        for i in range(16):
            if i == 8:
                drain()
            elif i > 0:
                drain(2)
            b = bank()

            def mmv(e, i=i, b=b, slot=slot):
                ins = None
                for kc in range(KC):
                    ins = e.matmul(PS[b], lhsT=hTb[:, kc, i * 128:(i + 1) * 128], rhs=slot[:, kc * 512:(kc + 1) * 512],
                                   start=(kc == 0), stop=(kc == KC - 1))
                return ins
            ops("pe", [rk] + [hbk(kc, i // 4) for kc in range(KC)], [pk(b)], mmv)
            dst = v_aug[:, i, :, 0:64]
            src = PS[b].rearrange("p (h e) -> p h e", h=8)
            if i % 2 == 0:
                ops("act", [pk(b)], [("v", i)], lambda e, dst=dst, src=src: e.activation(out=dst, in_=src, func=AF.Copy))
            else:
                ops("dve", [pk(b)], [("v", i)], lambda e, dst=dst, src=src: e.tensor_copy(out=dst, in_=src))

        for c in range(4):
            slot, rk = wnext(l, f"qk{c}")
            S.dma("pool", "tab", [], ["tab"],
                  lambda e, c=c: e.dma_start(out=tabt.rearrange("p h b q -> p (h b q)"), in_=tab_d[l, c],
                                             max_dma_last_dim=8192))
            for blk in range(4):
                b = bank()

                def mmq(e, b=b, blk=blk, slot=slot, o=0):
                    ins = None
                    for kc in range(KC):
                        ins = e.matmul(PS[b], lhsT=slot[:, kc * 256 + o:kc * 256 + o + 128], rhs=hTb[:, kc, bs(blk)],
                                       start=(kc == 0), stop=(kc == KC - 1))
                    return ins
                ops("pe", [rk] + [hbk(kc, blk) for kc in range(KC)], [pk(b)], mmq)
                ops("act", [pk(b), "bq8"], [("qA", blk)], lambda e, b=b, blk=blk, c=c: e.activation(
                    out=qA[0:64, bs(blk)], in_=PS[b][0:64, :], func=AF.Identity, scale=0.125, bias=bq8[0:64, l, c:c + 1]))
                ops("act", [pk(b), "bq8"], [("qB", blk)], lambda e, b=b, blk=blk, c=c: e.activation(
                    out=qB[64:128, bs(blk)], in_=PS[b][64:128, :], func=AF.Identity, scale=0.125,
                    bias=bq8[64:128, l, c:c + 1]))
                b2 = bank()
                ops("pe", [rk] + [hbk(kc, blk) for kc in range(KC)], [pk(b2)],
                    lambda e, b2=b2, blk=blk, slot=slot: mmq(e, b2, blk, slot, 128))
                ops("dve", [pk(b2), "par"], [("kT", blk)], lambda e, b2=b2, blk=blk, c=c: e.tensor_scalar(
                    out=kT[:, bs(blk)], in0=PS[b2], scalar1=pc(4 + c), scalar2=None, op0=ALU.add))

            def tile_info(i):
                if 2 <= i <= 13:
                    return list(range(i - 2, i + 3)), 0
                if i == 0:
                    return [0, 1, 2, 3], 5 + 3
                if i == 1:
                    return [0, 1, 2, 3], 5 + 2
                if i == 14:
                    return [12, 13, 14, 15], 5 + 1
                return [12, 13, 14, 15], 5 + 0

            def emit_S(i):
                s = i % 2
                J, tb0 = tile_info(i)
                qcols = slice(i * 128, (i + 1) * 128)
                for hh, (qq, qname) in enumerate(((qA, "qA"), (qB, "qB"))):
                    bnk = 2 * s + hh

                    def f(e, hh=hh, qq=qq, bnk=bnk, J=J, tb0=tb0):
                        ins = None
                        for jj in range(4):
                            ins = e.matmul(PS[bnk][:, jj * 128:(jj + 1) * 128],
                                           lhsT=kT[:, J[jj] * 128:(J[jj] + 1) * 128], rhs=qq[:, qcols],
                                           start=True, stop=True)
                        return ins
                    ops("pe", [(qname, i // 4)] + [("kT", j // 4) for j in J[:4]], [pk(2 * s + hh)], f)
                    ops("dve", [pk(bnk), "tab"], [pk(bnk)], lambda e, hh=hh, bnk=bnk, tb0=tb0: e.tensor_tensor(
                        out=PS[bnk], in0=PS[bnk], in1=tabt[:, hh, tb0:tb0 + 4, :].rearrange("p b q -> p (b q)"), op=ALU.add))
                if len(J) == 5:
                    def f5(e, J=J, tb0=tb0, s=s):
                        reg = PS[4 + s][:, 0:256]
                        e.matmul(reg[:, 0:128], lhsT=kT[:, J[4] * 128:(J[4] + 1) * 128], rhs=qA[:, qcols],
                                 start=True, stop=True)
                        return e.matmul(reg[:, 128:256], lhsT=kT[:, J[4] * 128:(J[4] + 1) * 128], rhs=qB[:, qcols],
                                        start=True, stop=True)
                    ops("pe", [("qA", i // 4), ("qB", i // 4), ("kT", J[4] // 4)], [pk(4 + s)], f5)
                    for hh in range(2):
                        ops("dve", [pk(4 + s), "tab"], [pk(4 + s)], lambda e, s=s, tb0=tb0, hh=hh: e.tensor_tensor(
                            out=PS[4 + s][:, hh * 128:(hh + 1) * 128], in0=PS[4 + s][:, hh * 128:(hh + 1) * 128],
                            in1=tabt[:, hh, tb0 + 4, :], op=ALU.add))
                for hh in range(2):
                    bnk = 2 * s + hh
                    ops("act", [pk(2 * s + hh)], [("PT", s, hh)], lambda e, hh=hh, bnk=bnk, s=s: e.activation(
                        out=PT[s][:, hh, 0:4, :].rearrange("p b q -> p (b q)"), in_=PS[bnk], func=AF.Exp))
                if len(J) == 5:
                    ops("act", [pk(4 + s)], [("PT", s, 2)], lambda e, s=s: e.activation(
                        out=PT[s][:, :, 4, :], in_=PS[4 + s][:, 0:256].rearrange("p (h q) -> p h q", h=2),
                        func=AF.Exp))

            def emit_PV(i):
                s = i % 2
                J, _ = tile_info(i)
                pvb = PS[6 + s]
                for hh in range(2):
                    def f(e, hh=hh, J=J, s=s, pvb=pvb):
                        ins = None
                        for jj, j in enumerate(J):
                            ins = e.matmul(pvb[:, hh * 65:(hh + 1) * 65], lhsT=PT[s][:, hh, jj, :],
                                           rhs=v_aug[:, j, 2 * c + hh, :], start=(jj == 0), stop=(jj == len(J) - 1))
                        return ins
                    rd = [("PT", s, hh)] + ([("PT", s, 2)] if len(J) == 5 else []) + [("v", j) for j in J]
                    ops("pe", rd, [pk(6 + s)], f)
                ops("dve", [pk(6 + s)], [("rden", s)], lambda e, s=s, pvb=pvb: e.reciprocal(
                    out=rden[s], in_=pvb[:, 0:130].rearrange("p (h e) -> p h e", h=2)[:, :, 64]))
                for hh in range(2):
                    ops("act", [pk(6 + s), ("rden", s)], [("otok", s)], lambda e, s=s, hh=hh, pvb=pvb: e.activation(
                        out=otok[s][:, hh * 64:(hh + 1) * 64], in_=pvb[:, hh * 65:hh * 65 + 64], func=AF.Identity,
                        scale=rden[s][:, hh:hh + 1]))

            def emit_T(i):
                s = i % 2
                fb = 4 + (i + 1) % 2
                ops("pe", [("otok", s), "identf"], [pk(fb)], lambda e, s=s, fb=fb: e.transpose(
                    out=PS[fb][:, 256:384], in_=otok[s], identity=identf))
                ops("dve", [pk(fb), "par"], [("oT", c, i // 4)], lambda e, i=i, fb=fb: e.tensor_scalar(
                    out=oT[:, c, i * 128:(i + 1) * 128], in0=PS[fb][:, 256:384], scalar1=pc(8 + c), scalar2=None,
                    op0=ALU.add))

            emit_S(0)
            for i in range(16):
                if i + 1 < 16:
                    emit_S(i + 1)
                emit_PV(i)
                if i >= 1:
                    emit_T(i - 1)
            emit_T(15)

        al.off = MIX_BASE
        HL = 16
        WB = T + 2 * HL
        ubuf = al.f32(WB)
        tA = al.f32(WB)
        tB = al.f32(WB)
        pooledT = al.bf16(T)
        t16 = al.f32(16)
        S.alias(ATT_KEYS, POOL_KEYS)
        slotU, rkU = wnext(l, "pin")
        slotW, rkW = wnext(l, "pw", live_prev=1)
        ops("dve", [], [("ub", b_) for b_ in range(4)], lambda e: e.memset(ubuf, 0.0))
        def uproj_pe(g):
            bl = []
            for blk in range(4):
                b = bank()
                bl.append(b)

                def mmu(e, b=b, blk=blk, g=g):
                    ins = None
                    for kc in range(KC):
                        ins = e.matmul(PS[b], lhsT=slotU[:, kc * 512 + g * 128:kc * 512 + (g + 1) * 128],
                                       rhs=hTb[:, kc, bs(blk)], start=(kc == 0), stop=(kc == KC - 1))
                    return ins
                ops("pe", [rkU] + [hbk(kc, blk) for kc in range(KC)], [pk(b)], mmu)
            return bl

        def uproj_evac(g, bl):
            for blk in range(4):
                ops("act", [pk(bl[blk]), "par"], [("ub", blk)], lambda e, b=bl[blk], blk=blk, g=g: e.activation(
                    out=ubuf[:, HL + blk * 512:HL + (blk + 1) * 512], in_=PS[b], func=AF.Identity, bias=pc(12 + g)))

        uproj_evac(0, uproj_pe(0))
        for g in range(4):
            w = 2 << g
            nxt = uproj_pe(g + 1) if g < 3 else None
            offs = [(-1, 0), (-1, 1), (-2, 2), (-4, 4)][:g + 1]
            rng_ = [(0, T)]
            for (o0, o1) in reversed(offs[1:]):
                lo, hi = rng_[0]
                rng_.insert(0, (lo + o0, hi + o1))
            src = ubuf
            srck = [("ub", b_) for b_ in range(4)]
            bufs = [tA, tB]
            bnames = ["tA", "tB"]
            for si, ((o0, o1), (lo, hi)) in enumerate(zip(offs, rng_)):
                dstb = bufs[si % 2]
                ops("dve", srck, [bnames[si % 2]], lambda e, src=src, dstb=dstb, o0=o0, o1=o1, lo=lo, hi=hi: e.tensor_tensor(
                    out=dstb[:, HL + lo:HL + hi], in0=src[:, HL + lo + o0:HL + hi + o0],
                    in1=src[:, HL + lo + o1:HL + hi + o1], op=ALU.add))
                src = dstb
                srck = [bnames[si % 2]]
            ops("dve", srck + [("ub", b_) for b_ in range(4)], ["pooledT"], lambda e, src=src, w=w: e.scalar_tensor_tensor(
                out=pooledT, in0=src[:, HL:HL + T], scalar=1.0 / w, in1=ubuf[:, HL:HL + T],
                op0=ALU.mult, op1=ALU.subtract))
            for (c0, r0) in ((0, 0), (T - 8, 8)):
                ops("dve", srck + ["par0"], ["t16"], lambda e, src=src, c0=c0, r0=r0, g=g: e.tensor_tensor(
                    out=t16[:, 0:8], in0=src[:, HL + c0:HL + c0 + 8], in1=par0[:, 16 + g * 16 + r0:16 + g * 16 + r0 + 8],
                    op=ALU.mult))
                ops("dve", ["t16"] + [("ub", b_) for b_ in range(4)], ["pooledT"], lambda e, c0=c0: e.tensor_tensor(
                    out=pooledT[:, c0:c0 + 8], in0=t16[:, 0:8], in1=ubuf[:, HL + c0:HL + c0 + 8], op=ALU.subtract),
                    self_sync=True)
            if nxt is not None:
                uproj_evac(g + 1, nxt)
            for blk in range(4):
                b = bank()
                ops("pe", [rkW, "pooledT"], [pk(b)], lambda e, b=b, blk=blk, g=g: e.matmul(
                    PS[b], lhsT=slotW[:, g * 128:(g + 1) * 128], rhs=pooledT[:, bs(blk)], start=True, stop=True))
                ops("act", [pk(b), "par"], [("p2", g, blk)], lambda e, b=b, blk=blk, g=g: e.activation(
                    out=pooled2T[:, g, bs(blk)], in_=PS[b], func=AF.Identity, scale=pc(32 + g)))

        al.off = MIX_BASE
        merged = al.bf16(8 * 1024).rearrange("p (k t) -> p k t", k=8)
        sa = [al.f32(512) for _ in range(2)]
        sb = [al.f32(512) for _ in range(2)]
        lnb = ([al.bf16(512) for _ in range(3)], [al.bf16(512) for _ in range(4)], ln_tm, ln_ta)
        S.alias(POOL_KEYS, M2_KEYS)
        rr = 0
        for hf in range(2):
            for n in range(8):
                slot, rk = wnext(l, f"m{n}")
                for bi in range(2):
                    blk = 2 * hf + bi
                    r = rr % 2
                    rr += 1

                    def mmk(e, b, o, nk, rhs_fn, slot=slot):
                        ins = None
                        for kc in range(nk):
                            ins = e.matmul(PS[b], lhsT=slot[:, o + kc * 128:o + (kc + 1) * 128], rhs=rhs_fn(kc),
                                           start=(kc == 0), stop=(kc == nk - 1))
                        return ins
                    b0 = bank()
                    ops("pe", [rk] + [hbk(kc, blk) for kc in range(KC)], [pk(b0)],
                        lambda e, b0=b0, blk=blk: mmk(e, b0, 0, 8, lambda kc: hTb[:, kc, bs(blk)]))
                    ops("act", [pk(b0), "par"], [("sa", r)], lambda e, b0=b0, r=r, n=n: e.activation(
                        out=sa[r], in_=PS[b0], func=AF.Sigmoid, bias=pc(16 + n)))
                    b1 = bank()
                    ops("pe", [rk] + [("oT", kc, blk) for kc in range(4)], [pk(b1)],
                        lambda e, b1=b1, blk=blk: mmk(e, b1, 2048, 4, lambda kc: oT[:, kc, bs(blk)]))
                    ops("dve", [("sa", r), pk(b1)], [("sa", r)], lambda e, b1=b1, r=r: e.tensor_tensor(
                        out=sa[r], in0=sa[r], in1=PS[b1], op=ALU.mult))
                    b2 = bank()
                    ops("pe", [rk] + [hbk(kc, blk) for kc in range(KC)], [pk(b2)],
                        lambda e, b2=b2, blk=blk: mmk(e, b2, 1024, 8, lambda kc: hTb[:, kc, bs(blk)]))
                    ops("act", [pk(b2), "par"], [("sb", r)], lambda e, b2=b2, r=r, n=n: e.activation(
                        out=sb[r], in_=PS[b2], func=AF.Sigmoid, bias=pc(24 + n)))
                    b3 = bank()
                    ops("pe", [rk] + [("p2", kc, blk) for kc in range(4)], [pk(b3)],
                        lambda e, b3=b3, blk=blk: mmk(e, b3, 2560, 4, lambda kc: pooled2T[:, kc, bs(blk)]))
                    ops("dve", [("sb", r), pk(b3)], [("sb", r)], lambda e, b3=b3, r=r: e.tensor_tensor(
                        out=sb[r], in0=sb[r], in1=PS[b3], op=ALU.mult))
                    ops("dve", [("sa", r), ("sb", r)], [("mg", n, bi)], lambda e, r=r, n=n, bi=bi: e.tensor_tensor(
                        out=merged[:, n, bi * 512:(bi + 1) * 512], in0=sa[r], in1=sb[r], op=ALU.add))
                    drain(1)
            drain()
            for j in range(2):
                slot, rk = wnext(l, f"mix{j}")
                for nn in range(4):
                    n2 = 4 * j + nn
                    for bi in range(2):
                        blk = 2 * hf + bi
                        b = bank()

                        def mmx(e, b=b, nn=nn, bi=bi, slot=slot):
                            ins = None
                            for kc in range(KC):
                                ins = e.matmul(PS[b], lhsT=slot[:, kc * 512 + nn * 128:kc * 512 + (nn + 1) * 128],
                                               rhs=merged[:, kc, bi * 512:(bi + 1) * 512],
                                               start=(kc == 0), stop=(kc == KC - 1))
                            return ins
                        ops("pe", [rk] + [("mg", kc, bi) for kc in range(KC)], [pk(b)], mmx)
                        xk = hT32[:, n2, bs(blk)]
                        ops("dve", [pk(b), h32k(n2, blk)], [h32k(n2, blk)], lambda e, b=b, xk=xk: e.scalar_tensor_tensor(
                            out=xk, in0=xk, scalar=ALPHA, in1=PS[b], op0=ALU.mult, op1=ALU.add))
            run_smalls([ln_stats(2 * hf + bi, lnb) for bi in range(2)])
            if hf == 1:
                for kc in range(KC):
                    ops("dve", [h32k(kc, 2), ("t_a", 0)], [("htmp", kc)], lambda e, kc=kc: e.tensor_tensor(
                        out=halo_tmp[:, kc, :], in0=hT32[:, kc, 1024:1026], in1=ln_ta[0][:, 0:2], op=ALU.mult))
                for kc in range(KC):
                    ops("dve", [("htmp", kc), ("t_mean", 0)], [("htmp", kc)], lambda e, kc=kc: e.tensor_tensor(
                        out=halo_tmp[:, kc, :], in0=halo_tmp[:, kc, :], in1=ln_tm[0][:, 0:2], op=ALU.add),
                        self_sync=(kc == 0))
                for kc in range(KC):
                    ops("act", [("htmp", kc), "par"], ["halo_nx"], lambda e, kc=kc: e.activation(
                        out=halo_nx[:, kc, :], in_=halo_tmp[:, kc, :], func=AF.Identity, scale=pc(36 + kc), bias=pc(44 + kc)))
            for bi in range(2):
                ln_apply(2 * hf + bi, lambda kc, pc=pc: pc(36 + kc), lambda kc, pc=pc: pc(44 + kc), lnb, defer=True)
        if stop == 'm2':
            break

        al.off = PHASE_BASE
        aT = al.bf16(NF * 1024).rearrange("p (f t) -> p f t", f=NF)
        pT = al.bf16(2 * T).rearrange("p (k t) -> p k t", k=2)
        gbuf = [al.f32(1026) for _ in range(2)]
        cbuf = [al.f32(1024) for _ in range(2)]
        pst = [al.f32(2 * PLE).rearrange("p (a d) -> p a d", a=2) for _ in range(2)]
        sg = [gbuf[0][:, 0:512], gbuf[1][:, 0:512]]
        lnb = ([al.bf16(512) for _ in range(3)], [al.bf16(512) for _ in range(4)], ln_tm, ln_ta)
        S.alias(MIXER_KEYS + M2_KEYS, FFN_KEYS)
        ops("act", [hbk(kc, 1) for kc in range(KC)], ["halo_save"],
            lambda e: e.activation(out=halo_save, in_=hTb[:, :, 1022:1024], func=AF.Copy))
        def emit_p_pair(tp2):
            j = tp2 % 2
            S.dma("sp", f"pin{j}", [], [("pst", j)],
                  lambda e, tp2=tp2, j=j: e.dma_start(
                      out=pst[j], in_=p_d[l, tp2 * 256:(tp2 + 1) * 256, :].rearrange("(a p) d -> p a d", p=128)))
            b = bank()

            def trp(e, j=j, b=b):
                ins = None
                for a in range(2):
                    for k in range(2):
                        ins = e.transpose(out=PS[b][:, (2 * a + k) * 128:(2 * a + k + 1) * 128],
                                          in_=pst[j][:, a, k * 128:(k + 1) * 128], identity=identf)
                return ins
            ops("pe", [("pst", j), "identf"], [pk(b)], trp)
            for a in range(2):
                ti = 2 * tp2 + a
                ops("act", [pk(b)], [("pT", ti // 4)], lambda e, b=b, ti=ti, a=a: e.activation(
                    out=pT[:, :, ti * 128:(ti + 1) * 128],
                    in_=PS[b][:, a * 256:(a + 1) * 256].rearrange("p (k t) -> p k t", k=2), func=AF.Copy))
        rr = 0
        for hf in range(2):
            zc = 0 if hf == 0 else 1025
            for r in range(2):
                ops("dve", [], [("gbuf", r)], lambda e, r=r, zc=zc: e.memset(gbuf[r][:, zc:zc + 1], 0.0))
            for fp in range(11):
                slot, rk = wnext(l, f"up{fp}")
                for fi in range(2):
                    f_ = 2 * fp + fi
                    r = rr % 2
                    rr += 1
                    gb_ = gbuf[r]
                    cb_ = cbuf[r]
                    og = fi * 2048 + 1024
                    ov = fi * 2048
                    bG = []
                    for bi in range(2):
                        blk = 2 * hf + bi
                        b = bank()
                        bG.append(b)

                        def mmg(e, b=b, blk=blk, o=og, slot=slot):
                            ins = None
                            for kc in range(KC):
                                ins = e.matmul(PS[b], lhsT=slot[:, o + kc * 128:o + (kc + 1) * 128], rhs=hTb[:, kc, bs(blk)],
                                               start=(kc == 0), stop=(kc == KC - 1))
                            return ins
                        ops("pe", [rk] + [hbk(kc, blk) for kc in range(KC)], [pk(b)], mmg)
                    bV = []
                    for bi in range(2):
                        blk = 2 * hf + bi
                        b = bank()
                        bV.append(b)

                        def mmv2(e, b=b, blk=blk, o=ov, slot=slot):
                            ins = None
                            for kc in range(KC):
                                ins = e.matmul(PS[b], lhsT=slot[:, o + kc * 128:o + (kc + 1) * 128], rhs=hTb[:, kc, bs(blk)],
                                               start=(kc == 0), stop=(kc == KC - 1))
                            return ins
                        ops("pe", [rk] + [hbk(kc, blk) for kc in range(KC)], [pk(b)], mmv2)
                    bH = bank()

                    def mmh(e, bH=bH, o=og, slot=slot, hf=hf):
                        ins = None
                        for kc in range(KC):
                            rhs = halo_nx[:, kc, :] if hf == 0 else halo_save[:, kc, :]
                            ins = e.matmul(PS[bH][:, 0:2], lhsT=slot[:, o + kc * 128:o + (kc + 1) * 128], rhs=rhs,
                                           start=(kc == 0), stop=(kc == KC - 1))
                        return ins
                    ops("pe", [rk, "halo_save" if hf == 1 else "halo_nx"], [pk(bH)], mmh)
                    for bi in range(2):
                        ops("act", [pk(bG[bi])], [("gbuf", r)], lambda e, b=bG[bi], bi=bi, gb_=gb_: e.activation(
                            out=gb_[:, 1 + bi * 512:1 + (bi + 1) * 512], in_=PS[b], func=AF.Copy))
                    if hf == 0:
                        ops("act", [pk(bH)], [("gbuf", r)], lambda e, bH=bH, gb_=gb_: e.activation(
                            out=gb_[:, 1025:1026], in_=PS[bH][:, 0:1], func=AF.Copy))
                    else:
                        ops("act", [pk(bH)], [("gbuf", r)], lambda e, bH=bH, gb_=gb_: e.activation(
                            out=gb_[:, 0:1], in_=PS[bH][:, 1:2], func=AF.Copy))
                    for bi in range(2):
                        ops("act", [pk(bG[bi]), "par"], [pk(bG[bi])], lambda e, b=bG[bi], f_=f_: e.activation(
                            out=PS[b], in_=PS[b], func=AF.Identity, scale=pc(74 + f_), bias=pc(118 + f_)))
                    for bi in range(2):
                        ops("dve", [("gbuf", r), pk(bG[bi]), "par"], [pk(bG[bi])],
                            lambda e, b=bG[bi], bi=bi, gb_=gb_, f_=f_: e.scalar_tensor_tensor(
                                out=PS[b], in0=gb_[:, bi * 512:bi * 512 + 512], scalar=pc(52 + f_), in1=PS[b],
                                op0=ALU.mult, op1=ALU.add))
                        ops("dve", [("gbuf", r), pk(bG[bi]), "par"], [pk(bG[bi])],
                            lambda e, b=bG[bi], bi=bi, gb_=gb_, f_=f_: e.scalar_tensor_tensor(
                                out=PS[b], in0=gb_[:, 2 + bi * 512:2 + bi * 512 + 512], scalar=pc(96 + f_), in1=PS[b],
                                op0=ALU.mult, op1=ALU.add))
                        ops("act", [pk(bG[bi])], [("cbuf", r)], lambda e, b=bG[bi], bi=bi, cb_=cb_: e.activation(
                            out=cb_[:, bi * 512:(bi + 1) * 512], in_=PS[b], func=AF.Gelu))
                    for bi in range(2):
                        ops("dve", [("cbuf", r), pk(bV[bi])], [("aT", f_, bi)], lambda e, b=bV[bi], bi=bi, cb_=cb_, f_=f_: e.tensor_tensor(
                            out=aT[:, f_, bi * 512:(bi + 1) * 512], in0=cb_[:, bi * 512:(bi + 1) * 512], in1=PS[b], op=ALU.mult))
                    if hf == 0 and f_ < 16 and f_ % 2 == 1:
                        emit_p_pair(f_ // 2)
            rs = 0
            for n in range(8):
                slot, rk = wnext(l, f"dn{n}")
                for bi in range(2):
                    blk = 2 * hf + bi
                    r = rs % 2
                    rs += 1
                    bD = bank()

                    def mmd(e, bD=bD, bi=bi, slot=slot):
                        ins = None
                        for fc in range(NF):
                            ins = e.matmul(PS[bD], lhsT=slot[:, fc * 128:(fc + 1) * 128], rhs=aT[:, fc, bi * 512:(bi + 1) * 512],
                                           start=(fc == 0), stop=(fc == NF - 1))
                        return ins
                    ops("pe", [rk] + [("aT", fc, bi) for fc in range(NF)], [pk(bD)], mmd)
                    bP = bank()

                    def mmp(e, bP=bP, blk=blk, slot=slot):
                        ins = None
                        for kc in range(KC):
                            ins = e.matmul(PS[bP], lhsT=slot[:, 2816 + kc * 128:2816 + (kc + 1) * 128], rhs=hTb[:, kc, bs(blk)],
                                           start=(kc == 0), stop=(kc == KC - 1))
                        return ins
                    ops("pe", [rk] + [hbk(kc, blk) for kc in range(KC)], [pk(bP)], mmp)
                    bQ = bank()

                    def mmq2(e, bQ=bQ, blk=blk, slot=slot):
                        ins = None
                        for kc in range(2):
                            ins = e.matmul(PS[bQ], lhsT=slot[:, 3840 + kc * 128:3840 + (kc + 1) * 128],
                                           rhs=pT[:, kc, bs(blk)], start=(kc == 0), stop=(kc == 1))
                        return ins
                    ops("pe", [rk, ("pT", blk)], [pk(bQ)], mmq2)
                    ops("act", [pk(bP)], [("gbuf", r)], lambda e, bP=bP, r=r: e.activation(out=sg[r], in_=PS[bP], func=AF.Sigmoid))
                    ops("dve", [("gbuf", r), pk(bQ)], [("gbuf", r)], lambda e, bQ=bQ, r=r: e.tensor_tensor(
                        out=sg[r], in0=sg[r], in1=PS[bQ], op=ALU.mult))
                    xk = hT32[:, n, bs(blk)]
                    ops("dve", [pk(bD), h32k(n, blk)], [h32k(n, blk)], lambda e, bD=bD, xk=xk: e.scalar_tensor_tensor(
                        out=xk, in0=xk, scalar=ALPHA, in1=PS[bD], op0=ALU.mult, op1=ALU.add))
                    ops("dve", [("gbuf", r), h32k(n, blk)], [h32k(n, blk)], lambda e, r=r, xk=xk: e.tensor_tensor(
                        out=xk, in0=xk, in1=sg[r], op=ALU.add))
                    drain(1)
            drain()
            run_smalls([ln_stats(2 * hf + bi, lnb) for bi in range(2)])
            for bi in range(2):
                ln_apply(2 * hf + bi, lambda kc, pc=pc: pc(140 + kc), lambda kc, pc=pc: pc(148 + kc), lnb, defer=True)
        prev_keys = FFN_KEYS

    al.off = PHASE_BASE
    ost = [al.f32(D) for _ in range(2)]
    S.alias(prev_keys, [("ost", 0), ("ost", 1)])
    for i in range(16):
        if i == 8:
            drain()
        j = i % 2
        for hh in range(2):
            b = bank()

            def tro(e, b=b, i=i, hh=hh):
                ins = None
                for q in range(4):
                    kc = hh * 4 + q
                    ins = e.transpose(out=PS[b][:, q * 128:(q + 1) * 128], in_=hT32[:, kc, i * 128:(i + 1) * 128],
                                      identity=identf)
                return ins
            ops("pe", [h32k(hh * 4 + q, i // 4) for q in range(4)] + ["identf"], [pk(b)], tro)
            if hh == 0:
                ops("act", [pk(b)], [("ost", j)], lambda e, b=b, j=j: e.activation(out=ost[j][:, 0:512], in_=PS[b], func=AF.Copy))
            else:
                ops("dve", [pk(b)], [("ost", j)], lambda e, b=b, j=j: e.tensor_copy(out=ost[j][:, 512:1024], in_=PS[b]))
        S.dma("sp", f"out{j}", [("ost", j)], [],
              lambda e, i=i, j=j: e.dma_start(out=y_d[i * 128:(i + 1) * 128, :], in_=ost[j]))
    S.final_wait("sp", ["d:out0", "d:out1"])
    return nc


_CACHE = {}


def _prepare(inputs, depth):
    ws, units = build_weight_stream(inputs, depth)
    par, par0 = build_params(inputs, depth)
    tab = build_bias_tables(inputs["rpb"], depth)
    return ws, units, par, par0, tab


def run(inputs, depth=DEPTH, cores=8, trace=False, stop=None):
    inputs = {k: np.asarray(v) for k, v in inputs.items()}
    ws, units, par, par0, tab = _prepare(inputs, depth)
    nc = build_nc(depth, units, ws.size, stop=stop)
    idn = np.eye(128, dtype=np.float32)
    x = inputs["x"]
    p = inputs["p"]
    in_maps = []
    for c in range(cores):
        in_maps.append({
            "x": np.ascontiguousarray(x[c]), "p": np.ascontiguousarray(p[:depth, c]),
            "ws": ws, "tab": tab, "par": par, "par0": par0, "idn": idn,
        })
    res = run_bass_kernel_spmd(nc, in_maps, core_ids=list(range(cores)), trace=trace)
    out = np.stack([res.results[c]["y"] for c in range(cores)], axis=0)
    return out, res


def kernel(**inputs):
    out, _ = run(inputs, DEPTH, 8)
    return out.astype(np.float32)
```
